# Optimizing a Trainium2 kernel written in Bass

```python
import math
import jax, jax.numpy as jnp
from jax import lax
import numpy as np

D_MODEL = 2048
BATCH = 4
SEQ = 2048
DEPTH = 1
DEC_BATCH = 8
DEC_SEQ = 32
PAST_LEN = 1024

CHUNK = 64
HEAD_DIM = 64
ATT_HEADS = D_MODEL // 2 // HEAD_DIM
ATT_KV_HEADS = 2
ATT_GROUP = ATT_HEADS // ATT_KV_HEADS
ATT_WIDTH = ATT_HEADS * HEAD_DIM
KV_WIDTH = ATT_KV_HEADS * HEAD_DIM
WINDOW = 128
WINDOW_CHUNKS = WINDOW // CHUNK
ROPE_THETA = 10000.0
ATT_SCALE = HEAD_DIM ** -0.5
RW_HEADS = D_MODEL // 2 // HEAD_DIM
RW_WIDTH = RW_HEADS * HEAD_DIM
DECAY_LORA = 64
AAA_LORA = 64
GATE_LORA = 160
LNX_EPS = 64e-5
ATT_SIZES = (ATT_WIDTH, KV_WIDTH, KV_WIDTH)
RW_SIZES = (RW_WIDTH, DECAY_LORA, RW_WIDTH, RW_WIDTH, AAA_LORA, GATE_LORA)
ATT_COLS = ATT_WIDTH + 2 * KV_WIDTH
RW_COLS = 3 * RW_WIDTH + DECAY_LORA + AAA_LORA + GATE_LORA
IN_COLS = ATT_COLS + RW_COLS
MIX_WIDTH = ATT_WIDTH + RW_WIDTH
D_FF = 5632
CONV_W = 3
LN_EPS = 1e-5
ALPHA = (2 * DEPTH) ** 0.25
BETA = (8 * DEPTH) ** -0.25

kernel_name = 'hybrid_swa_sink_rwkv7_convffn_stream_step'


def layer_norm(x, g, b, eps=LN_EPS):
    xf = x.astype(jnp.float32)
    mu = jnp.mean(xf, -1, keepdims=True)
    var = jnp.mean(jnp.square(xf - mu), -1, keepdims=True)
    return ((xf - mu) * lax.rsqrt(var + eps) * g + b).astype(x.dtype)


def split_cols(p, sizes):
    out, start = [], 0
    for s in sizes:
        out.append(p[..., start:start + s])
        start += s
    return out


def rope(x, pos):
    half = HEAD_DIM // 2
    inv = ROPE_THETA ** (-jnp.arange(half, dtype=jnp.float32) / half)
    ang = pos.astype(jnp.float32)[:, None] * inv[None, :]
    cos = jnp.cos(ang)[None, :, None, :]
    sin = jnp.sin(ang)[None, :, None, :]
    xf = x.astype(jnp.float32)
    x1, x2 = xf[..., :half], xf[..., half:]
    return jnp.concatenate([x1 * cos - x2 * sin, x2 * cos + x1 * sin], -1).astype(x.dtype)


def sink_softmax(s, sink):
    m = jnp.maximum(jnp.max(s, -1), sink)
    p = jnp.exp(s - m[..., None])
    den = jnp.sum(p, -1) + jnp.exp(sink - m)
    return p / den[..., None]


def band_attention_prompt(q, k, v, sinks):
    B, T = q.shape[0], q.shape[1]
    nc = T // CHUNK
    qc = q.reshape(B, nc, CHUNK, ATT_KV_HEADS, ATT_GROUP, HEAD_DIM)

    def band(t):
        tc = t.reshape(B, nc, CHUNK, ATT_KV_HEADS, HEAD_DIM)
        tp = jnp.pad(tc, ((0, 0), (WINDOW_CHUNKS, 0), (0, 0), (0, 0), (0, 0)))
        return jnp.concatenate([tp[:, i:i + nc] for i in range(WINDOW_CHUNKS + 1)], axis=2)

    kb, vb = band(k), band(v)
    slot_chunk = jnp.repeat(jnp.arange(WINDOW_CHUNKS + 1), CHUNK)
    valid = (jnp.arange(nc)[:, None] - WINDOW_CHUNKS + slot_chunk[None, :]) >= 0
    s = jnp.einsum('bnqkgd,bnskd->bnkgqs', qc, kb, preferred_element_type=jnp.float32) * ATT_SCALE
    s = jnp.where(valid[None, :, None, None, None, :], s, -jnp.inf)
    p = sink_softmax(s, sinks.astype(jnp.float32).reshape(ATT_KV_HEADS, ATT_GROUP, 1))
    o = jnp.einsum('bnkgqs,bnskd->bnqkgd', p.astype(v.dtype), vb)
    return o.reshape(B, T, ATT_WIDTH)


def window_attention_step(q, k_new, v_new, k_buf, v_buf, sinks):
    B, T = q.shape[0], q.shape[1]
    k_all = jnp.concatenate([k_buf, k_new], axis=1)
    v_all = jnp.concatenate([v_buf, v_new], axis=1)
    qg = q.reshape(B, T, ATT_KV_HEADS, ATT_GROUP, HEAD_DIM)
    s = jnp.einsum('bqkgd,bskd->bkgqs', qg, k_all, preferred_element_type=jnp.float32) * ATT_SCALE
    p = sink_softmax(s, sinks.astype(jnp.float32).reshape(ATT_KV_HEADS, ATT_GROUP, 1))
    o = jnp.einsum('bkgqs,bskd->bqkgd', p.astype(v_all.dtype), v_all)
    return o.reshape(B, T, ATT_WIDTH), k_all[:, -WINDOW:], v_all[:, -WINDOW:]


def wkv7_scan(r, decay, k, v, a, b, s0):
    def step(S, inp):
        r_t, w_t, k_t, v_t, a_t, b_t = inp
        sa = jnp.einsum('bhij,bhj->bhi', S, a_t)
        S = S * w_t[:, :, None, :] + sa[..., None] * b_t[:, :, None, :] + v_t[..., None] * k_t[:, :, None, :]
        return S, jnp.einsum('bhij,bhj->bhi', S, r_t)

    xs = tuple(jnp.moveaxis(t, 1, 0) for t in (r, decay, k, v, a, b))
    s_last, out = lax.scan(step, s0, xs)
    return jnp.moveaxis(out, 0, 1), s_last


def token_mixer(x, pos, shift_prev, wkv0, k_buf, v_buf, lp):
    B, T = x.shape[0], x.shape[1]
    p = x @ lp['w_in']
    p_att, p_rw = p[..., :ATT_COLS], p[..., ATT_COLS:]
    q, k, v = split_cols(p_att, ATT_SIZES)
    q = rope(q.reshape(B, T, ATT_HEADS, HEAD_DIM), pos)
    k = rope(k.reshape(B, T, ATT_KV_HEADS, HEAD_DIM), pos)
    v = v.reshape(B, T, ATT_KV_HEADS, HEAD_DIM)
    if k_buf is None:
        att = band_attention_prompt(q, k, v, lp['sinks'])
        k_win, v_win = k[:, -WINDOW:], v[:, -WINDOW:]
    else:
        att, k_win, v_win = window_attention_step(q, k, v, k_buf, v_buf, lp['sinks'])
    prev = jnp.concatenate([shift_prev, p_rw[:, :-1]], axis=1)
    xs = p_rw + (prev - p_rw) * lp['mu']
    r, wd, kr, vr, ad, gd = split_cols(xs, RW_SIZES)
    w_log = -jax.nn.softplus(-(lp['w0'] + jnp.tanh(wd) @ lp['w2'])) - 0.5
    decay = jnp.exp(-jnp.exp(w_log.astype(jnp.float32)))
    a = jax.nn.sigmoid(lp['a0'] + ad @ lp['a2'])
    g = jax.nn.sigmoid(gd) @ lp['g2']

    def heads(t):
        return t.reshape(B, T, RW_HEADS, HEAD_DIM).astype(jnp.float32)

    kk = heads(kr * lp['k_k'])
    kk = kk / jnp.maximum(jnp.sqrt(jnp.sum(kk * kk, -1, keepdims=True)), 1e-12)
    kr = kr * (1.0 + (a - 1.0) * lp['k_a'])
    rh, kh, vh, ah = heads(r), heads(kr), heads(vr), heads(a)
    o, wkv_last = wkv7_scan(rh, heads(decay), kh, vh, -kk, kk * ah, wkv0.astype(jnp.float32))
    mo = jnp.mean(o, -1, keepdims=True)
    vo = jnp.mean(jnp.square(o - mo), -1, keepdims=True)
    on = ((o - mo) * lax.rsqrt(vo + LNX_EPS)).reshape(B, T, RW_WIDTH) * lp['lnx_g'] + lp['lnx_b']
    bonus = jnp.sum(rh * kh * lp['r_k'].astype(jnp.float32), -1, keepdims=True) * vh
    rw_out = ((on + bonus.reshape(B, T, RW_WIDTH)) * g).astype(x.dtype)
    mixed = jnp.concatenate([att.astype(x.dtype), rw_out], axis=-1) @ lp['w_out']
    return mixed, k_win, v_win, wkv_last.astype(wkv0.dtype), p_rw[:, -1:]


def conv_ffn(x, conv_prev, lp):
    T = x.shape[1]
    up = x @ lp['w_up']
    ext = jnp.concatenate([conv_prev, up], axis=1)
    c = lp['conv_b']
    for i in range(CONV_W):
        c = c + ext[:, i:i + T] * lp['conv_w'][i]
    gate, val = c[..., :D_FF], c[..., D_FF:]
    y = (jax.nn.gelu(gate, approximate=False) * val) @ lp['w_down']
    return y, ext[:, -(CONV_W - 1):]


def setup_inputs(seed: int = 0) -> dict:
    key = jax.random.key(seed)
    ks = jax.random.split(key, 32)
    f32 = jnp.float32
    nrm = lambda k, shape, s=1.0: (jax.random.normal(k, shape, f32) * s)
    return {
        'x_prompt': nrm(ks[0], (BATCH, SEQ, D_MODEL)),
        'x_sample': nrm(ks[1], (DEC_BATCH, DEC_SEQ, D_MODEL)),
        'cache_k': nrm(ks[2], (DEPTH, DEC_BATCH, WINDOW, ATT_KV_HEADS, HEAD_DIM)),
        'cache_v': nrm(ks[3], (DEPTH, DEC_BATCH, WINDOW, ATT_KV_HEADS, HEAD_DIM)),
        'state_wkv': nrm(ks[4], (DEPTH, DEC_BATCH, RW_HEADS, HEAD_DIM, HEAD_DIM), 0.5),
        'state_shift': nrm(ks[5], (DEPTH, DEC_BATCH, 1, RW_COLS)),
        'state_ffn_conv': nrm(ks[6], (DEPTH, DEC_BATCH, CONV_W - 1, 2 * D_FF)),
        'ln_in_g': 1.0 + nrm(ks[7], (D_MODEL,), 0.02),
        'ln_in_b': nrm(ks[8], (D_MODEL,), 0.02),
        'w_in': nrm(ks[9], (DEPTH, D_MODEL, IN_COLS), D_MODEL ** -0.5),
        'attn_sinks': nrm(ks[10], (DEPTH, ATT_HEADS)),
        'rw_mu': jax.random.uniform(ks[11], (DEPTH, RW_COLS), f32),
        'rw_w0': jax.random.uniform(ks[12], (DEPTH, RW_WIDTH), f32, -6.0, -1.0),
        'rw_w2': nrm(ks[13], (DEPTH, DECAY_LORA, RW_WIDTH), 0.1 * DECAY_LORA ** -0.5),
        'rw_a0': nrm(ks[14], (DEPTH, RW_WIDTH), 0.1),
        'rw_a2': nrm(ks[15], (DEPTH, AAA_LORA, RW_WIDTH), 0.1 * AAA_LORA ** -0.5),
        'rw_g2': nrm(ks[16], (DEPTH, GATE_LORA, RW_WIDTH), GATE_LORA ** -0.5),
        'rw_k_k': 0.85 + nrm(ks[17], (DEPTH, RW_WIDTH), 0.02),
        'rw_k_a': 1.0 + nrm(ks[18], (DEPTH, RW_WIDTH), 0.02),
        'rw_r_k': nrm(ks[19], (DEPTH, RW_HEADS, HEAD_DIM), 0.1),
        'rw_lnx_g': 1.0 + nrm(ks[20], (DEPTH, RW_WIDTH), 0.02),
        'rw_lnx_b': nrm(ks[21], (DEPTH, RW_WIDTH), 0.02),
        'w_out': nrm(ks[22], (DEPTH, MIX_WIDTH, D_MODEL), BETA * MIX_WIDTH ** -0.5),
        'ln1_g': 1.0 + nrm(ks[23], (DEPTH, D_MODEL), 0.02),
        'ln1_b': nrm(ks[24], (DEPTH, D_MODEL), 0.02),
        'ffn_w_up': nrm(ks[25], (DEPTH, D_MODEL, 2 * D_FF), D_MODEL ** -0.5),
        'ffn_conv_w': nrm(ks[26], (DEPTH, CONV_W, 2 * D_FF), CONV_W ** -0.5),
        'ffn_conv_b': nrm(ks[27], (DEPTH, 2 * D_FF), 0.02),
        'ffn_w_down': nrm(ks[28], (DEPTH, D_FF, D_MODEL), BETA * D_FF ** -0.5),
        'ln2_g': 1.0 + nrm(ks[29], (DEPTH, D_MODEL), 0.02),
        'ln2_b': nrm(ks[30], (DEPTH, D_MODEL), 0.02),
    }


def reference(x_prompt, x_sample, cache_k, cache_v, state_wkv, state_shift, state_ffn_conv,
              ln_in_g, ln_in_b, w_in, attn_sinks, rw_mu, rw_w0, rw_w2, rw_a0, rw_a2, rw_g2,
              rw_k_k, rw_k_a, rw_r_k, rw_lnx_g, rw_lnx_b, w_out, ln1_g, ln1_b,
              ffn_w_up, ffn_conv_w, ffn_conv_b, ffn_w_down, ln2_g, ln2_b):
    def layer_params(l):
        return {'w_in': w_in[l], 'sinks': attn_sinks[l], 'mu': rw_mu[l], 'w0': rw_w0[l], 'w2': rw_w2[l],
                'a0': rw_a0[l], 'a2': rw_a2[l], 'g2': rw_g2[l], 'k_k': rw_k_k[l], 'k_a': rw_k_a[l],
                'r_k': rw_r_k[l], 'lnx_g': rw_lnx_g[l], 'lnx_b': rw_lnx_b[l], 'w_out': w_out[l],
                'w_up': ffn_w_up[l], 'conv_w': ffn_conv_w[l], 'conv_b': ffn_conv_b[l], 'w_down': ffn_w_down[l]}

    def run(x, pos, layer_states):
        h = layer_norm(x, ln_in_g, ln_in_b)
        ks_, vs_, ws_, ss_, cs_ = [], [], [], [], []
        for l in range(DEPTH):
            ck, cv, cw, csh, cf = layer_states[l]
            lp = layer_params(l)
            m, kb, vb, wkv, sh = token_mixer(h, pos, csh, cw, ck, cv, lp)
            h = layer_norm(ALPHA * h + m, ln1_g[l], ln1_b[l])
            f, cb = conv_ffn(h, cf, lp)
            h = layer_norm(ALPHA * h + f, ln2_g[l], ln2_b[l])
            ks_.append(kb); vs_.append(vb); ws_.append(wkv); ss_.append(sh); cs_.append(cb)
        return h, jnp.stack(ks_), jnp.stack(vs_), jnp.stack(ws_), jnp.stack(ss_), jnp.stack(cs_)

    bp, tp = x_prompt.shape[0], x_prompt.shape[1]
    prompt_states = [(None, None,
                      jnp.zeros((bp, RW_HEADS, HEAD_DIM, HEAD_DIM), jnp.float32),
                      jnp.zeros((bp, 1, RW_COLS), x_prompt.dtype),
                      jnp.zeros((bp, CONV_W - 1, 2 * D_FF), x_prompt.dtype)) for _ in range(DEPTH)]
    y_prompt, p_k, p_v, p_wkv, p_shift, p_conv = run(x_prompt, jnp.arange(tp, dtype=jnp.int32), prompt_states)

    ts = x_sample.shape[1]
    sample_states = [(cache_k[l], cache_v[l], state_wkv[l], state_shift[l], state_ffn_conv[l]) for l in range(DEPTH)]
    y_sample, s_k, s_v, s_wkv, s_shift, s_conv = run(
        x_sample, PAST_LEN + jnp.arange(ts, dtype=jnp.int32), sample_states)
    return (y_prompt, y_sample, p_k, p_v, p_wkv, p_shift, p_conv, s_k, s_v, s_wkv, s_shift, s_conv)
```

```python
from contextlib import ExitStack
import numpy as np
import concourse.bass as bass
import concourse.mybir as mybir
from concourse.bass_utils import run_bass_kernel_spmd

F32 = mybir.dt.float32
BF16 = mybir.dt.bfloat16
AF = mybir.ActivationFunctionType
ALU = mybir.AluOpType
AX = mybir.AxisListType

ENGS = ("pe", "act", "dve", "pool", "sp")
EPOCH = 12000

D = 2048
NT = 2048
GT = 512
NG = 4
TS = 32
DFF = 5632
NFT = 88
ALPHA = 2 ** 0.25


def _bf16_round(x):
    u = int(np.array([x], np.float32).view(np.uint32)[0])
    r = ((u + 0x7FFF + ((u >> 16) & 1)) >> 16) << 16
    return float(np.array([r], np.uint32).view(np.float32)[0])


ALPHA_HI = _bf16_round(ALPHA)
ALPHA_LO = ALPHA - ALPHA_HI
ATT_SCALE = 0.125
LN_EPS = 1e-5
LNX_EPS = 64e-5
C0 = float(np.exp(-0.5))
NRW = 27


class Buf:
    __slots__ = ("name", "w", "r", "dsem")

    def __init__(self, name, grave=None):
        self.name = name
        self.w = None
        self.r = dict(grave) if grave else {}
        self.dsem = None


class Sched:
    def __init__(self, nc):
        self.nc = nc
        self.streams = {e: [] for e in ENGS}
        self.count = {}
        self.waited = {e: {} for e in ENGS}
        self.sems = {}
        self.ctx = []
        self.epoch = {e: 0 for e in ENGS}
        for e in ENGS:
            self._mksem(("E", e, 0), "sem_%s0" % e)
        self.ndsem = 0
        self.final = []
        self.ninst = {e: 0 for e in ENGS}

    def _mksem(self, key, name):
        cm = self.nc.semaphore(name)
        h = cm.__enter__()
        self.ctx.append(cm)
        self.sems[key] = h
        self.count[key] = 0
        return h

    def ekey(self, eng):
        key = ("E", eng, self.epoch[eng])
        if self.count[key] >= EPOCH:
            self.epoch[eng] += 1
            key = ("E", eng, self.epoch[eng])
            self._mksem(key, "sem_%s%d" % (eng, self.epoch[eng]))
        return key

    def dkey(self, buf):
        if buf.dsem is None or self.count[buf.dsem] >= 2 * EPOCH:
            key = ("D", self.ndsem)
            self.ndsem += 1
            self._mksem(key, "dsem%d" % key[1])
            buf.dsem = key
        return buf.dsem

    def _needs(self, eng, reads, writes):
        need = {}
        wd = self.waited[eng]

        def add(kc):
            k, c = kc
            if k[0] == "E" and k[1] == "pe" and eng == "pe":
                return
            if wd.get(k, 0) >= c:
                return
            if need.get(k, 0) < c:
                need[k] = c
        for b in reads:
            if b.w is not None:
                add(b.w)
        for b in writes:
            if b.w is not None:
                add(b.w)
            for k, c in b.r.items():
                add((k, c))
        st = self.streams[eng]
        for k, c in need.items():
            wd[k] = c
            sem = self.sems[k]
            st.append(lambda e, sem=sem, c=c: e.wait_ge(sem, c))

    def _mark(self, key, c, reads, writes):
        for b in reads:
            if b.r.get(key, 0) < c:
                b.r[key] = c
        for b in writes:
            b.w = (key, c)
            b.r = {}

    def op(self, eng, fn, reads=(), writes=()):
        self._needs(eng, reads, writes)
        key = self.ekey(eng)
        self.count[key] += 1
        c = self.count[key]
        sem = self.sems[key]
        rec = _Rec()
        fn(rec)
        calls = rec.calls
        self.streams[eng].append(lambda e, calls=calls, sem=sem: _replay(e, calls).then_inc(sem, 1))
        self.ninst[eng] += 1
        self._mark(key, c, reads, writes)

    def dma(self, eng, out_ap, in_ap, reads=(), writes=(), track=None, final=False):
        self._needs(eng, reads, writes)
        tb = track if track is not None else (writes[0] if writes else reads[0])
        key = self.dkey(tb)
        self.count[key] += 16
        c = self.count[key]
        sem = self.sems[key]
        self.streams[eng].append(lambda e, o=out_ap, i=in_ap, sem=sem: e.dma_start(out=o, in_=i).then_inc(sem, 16))
        self.ninst[eng] += 1
        self._mark(key, c, reads, writes)
        if final:
            self.final.append((key, c))

    def finish(self):
        fin = {}
        for k, c in self.final:
            fin[k] = max(fin.get(k, 0), c)
        for k, c in fin.items():
            sem = self.sems[k]
            self.streams["sp"].append(lambda e, sem=sem, c=c: e.wait_ge(sem, c))
        streams = self.streams
        with self.nc.Block() as block:
            @block.tensor
            def _(e):
                for f in streams["pe"]:
                    f(e)

            @block.scalar
            def _(e):
                for f in streams["act"]:
                    f(e)

            @block.vector
            def _(e):
                for f in streams["dve"]:
                    f(e)

            @block.gpsimd
            def _(e):
                for f in streams["pool"]:
                    f(e)

            @block.sync
            def _(e):
                for f in streams["sp"]:
                    f(e)
        for cm in reversed(self.ctx):
            cm.__exit__(None, None, None)


class _Rec:
    def __init__(self):
        self.calls = []

    def __getattr__(self, name):
        def f(*a, **k):
            self.calls.append((name, a, k))
            return self
        return f


def _replay(e, calls):
    r = None
    for name, a, k in calls:
        r = getattr(e, name)(*a, **k)
    return r


class T:
    def __init__(self, h, b):
        self.h = h
        self.b = b

    def __getitem__(self, k):
        return self.h[k]


class Seg:
    def __init__(self, name, T_, C, off):
        self.name = name
        self.T = T_
        self.C = C
        self.nch = T_ // C
        self.off = off
        self.rows = [min(128, T_ - i * 128) for i in range((T_ + 127) // 128)]
        self.ntile = len(self.rows)


def build(debug=None, ngroups=NG, stop=None):
    nc = bass.Bass("TRN2", target_bir_lowering=False)
    S = Sched(nc)
    root = ExitStack()
    stack = [root]
    grave = {}
    scope_bufs = [[]]
    dbg_out = {}
    uid = [0]

    def din(name, shape):
        return nc.dram_tensor(name, list(shape), F32, kind="ExternalInput").ap()

    def dout(name, shape):
        return nc.dram_tensor(name, list(shape), F32, kind="ExternalOutput").ap()

    def sb(name, shape, dt=F32):
        uid[0] += 1
        name = "%s_u%d" % (name, uid[0])
        h = stack[-1].enter_context(nc.sbuf_tensor(name, list(shape), dt))
        b = Buf(name, grave)
        scope_bufs[-1].append(b)
        return T(h, b)

    class scope:
        def __enter__(self):
            st = ExitStack()
            stack.append(st)
            scope_bufs.append([])
            return self

        def __exit__(self, *a):
            for b in scope_bufs.pop():
                for kc in ([b.w] if b.w else []) + list(b.r.items()):
                    if grave.get(kc[0], 0) < kc[1]:
                        grave[kc[0]] = kc[1]
            stack.pop().close()
            return False

    def op(eng, fn, r=(), w=()):
        S.op(eng, fn, [t.b for t in r], [t.b for t in w])

    def dma(eng, o, i, r=(), w=(), final=False):
        S.dma(eng, o, i, [t.b for t in r], [t.b for t in w], final=final)

    def dump(name, t, ap, shape):
        if debug is None or name not in debug:
            return
        o = dout("dbg_" + name, shape)
        dbg_out[name] = shape
        dma("pool", o, ap, r=[t], final=True)

    xseq = din("xseq", [NT, D])
    xsmp = din("xsmp", [TS, D])
    flag_d = din("flag", [128, 1])
    cs_p = din("cs_p", [NT, 128])
    cs_s = din("cs_s", [TS, 128])
    cache_k = din("cache_k", [128, 128])
    cache_v = din("cache_v", [128, 128])
    swkv_in = din("swkv_in", [8, 128, 64])
    sshift_in = din("sshift_in", [128, NRW])
    sconv_in = din("sconv_in", [128, NFT, 2])
    w_att = din("w_att", [5, 128, 16, 256])
    w_rw = din("w_rw", [NRW, 128, 16, 128])
    w_outd = din("w_out", [4, 128, 16, 512])
    w_up = din("w_up", [NFT, 128, 16, 128])
    w_dn = din("w_dn", [11, 128, 4, 2048])
    vecs = din("vecs", [128, 160])
    lora_wa = din("lora_wa", [128, 1024])
    lora_g = din("lora_g", [128, 1024])
    lora_gb = din("lora_gb", [32, 1024])
    sinks_d = din("sinks", [16])
    ln2g_d = din("ln2g", [D])
    ln2b_d = din("ln2b", [D])
    convw_d = din("convw", [128, NFT, 3])
    convb_d = din("convb", [128, NFT])
    consts = din("consts", [128, 1024])

    y_main = dout("y_main", [1024, D])
    y_s = dout("y_s", [TS, D])
    pk_o = dout("pk", [128, 128])
    pv_o = dout("pv", [128, 128])
    sk_o = dout("sk", [128, 128])
    sv_o = dout("sv", [128, 128])
    pwkv_o = dout("pwkv", [8, 128, 64])
    swkv_o = dout("swkv", [8, 128, 64])
    pshift_o = dout("pshift", [128, NRW])
    sshift_o = dout("sshift", [128, NRW])
    pconv_o = dout("pconv", [128, NFT, 2])
    sconv_o = dout("sconv", [128, NFT, 2])

    cst = sb("cst", [128, 1024])
    dma("sp", cst[:], consts, w=[cst])
    identf = cst[:, 0:128]
    bones = cst[:, 128:256]
    mL = cst[:, 256:320]
    mUI = cst[:, 320:448]
    rmask = cst[:, 448:960]
    cb = sb("cb", [128, 512], BF16)
    op("dve", lambda e: e.tensor_copy(cb[:, 0:128], cst[:, 0:128]), r=[cst], w=[cb])
    op("dve", lambda e: e.tensor_scalar(cb[:, 128:256], cst[:, 0:128], ALPHA, None, ALU.mult), r=[cst], w=[cb])
    op("dve", lambda e: e.memset(cb[:, 256:384], 1.0), w=[cb])
    op("dve", lambda e: e.tensor_scalar(cb[:, 384:512], cst[:, 0:128], ALPHA_LO, None, ALU.mult), r=[cst], w=[cb])
    op("dve", lambda e: e.tensor_tensor(cb[:, 320:384], cst[:, 0:64], cst[:, 64:128], ALU.add), r=[cst], w=[cb])
    identb = cb[:, 0:128]
    aidentb = cb[:, 128:256]
    aidentb_lo = cb[:, 384:512]
    onesb = cb[:, 256:320]
    bones64 = sb("bones64", [128, 128])
    op("dve", lambda e: e.tensor_scalar(bones64[:], cst[:, 128:256], 1.0 / 64, None, ALU.mult), r=[cst], w=[bones64])
    vec = sb("vec", [128, 160])
    dma("sp", vec[:], vecs, w=[vec])
    V_LNG, V_LNB, V_L1G, V_L1B = 0, 16, 32, 48
    V_MU, V_W0, V_A0, V_KK, V_KA, V_RK, V_XG, V_XB = 64, 91, 99, 107, 115, 123, 131, 139
    omu = sb("omu", [128, NRW])
    nka = sb("nka", [128, 8])
    flag = sb("flagt", [128, 1])
    dma("sp", flag[:], flag_d, w=[flag])
    lwa = sb("lwa", [128, 1024], BF16)
    lg = sb("lg", [128, 1024], BF16)
    lgb = sb("lgb", [32, 1024], BF16)
    dma("pool", lwa[:], lora_wa, w=[lwa])
    dma("pool", lg[:], lora_g, w=[lg])
    dma("pool", lgb[:], lora_gb, w=[lgb])
    es = sb("es", [128, 16])
    dma("sp", es[:], sinks_d.partition_broadcast(128), w=[es])
    op("act", lambda e: e.activation(es[:], es[:], AF.Exp), r=[es], w=[es])
    esb = sb("esb", [128, 2, 4, 64])
    for g in range(2):
        for hh in range(2):
            for i in range(4):
                hd = 8 * g + 4 * hh + i
                op("dve", lambda e, g=g, hh=hh, i=i, hd=hd: e.tensor_copy(
                    esb[hh * 64:(hh + 1) * 64, g, i, :], es[hh * 64:(hh + 1) * 64, hd:hd + 1].to_broadcast([64, 64])),
                   r=[es], w=[esb])
    cvw = sb("cvw", [128, NFT, 3])
    cvb = sb("cvb", [128, NFT])
    dma("sp", cvw[:], convw_d, w=[cvw])
    dma("sp", cvb[:], convb_d, w=[cvb])
    h1halo = sb("h1halo", [128, 16, 2], BF16)

    banks = []
    for i in range(8):
        h = root.enter_context(nc.psum_tensor("pb%d" % i, [128, 512], F32))
        banks.append(T(h, Buf("pb%d" % i)))
    bank_i = [0]

    def bank():
        t = banks[bank_i[0] % 8]
        bank_i[0] += 1
        return t

    class Slots:
        def __init__(self, name, shape, n, dt=BF16):
            self.t = [sb("%s%d" % (name, i), shape, dt) for i in range(n)]
            self.i = 0

        def next(self):
            t = self.t[self.i % len(self.t)]
            self.i += 1
            return t

    wsl = Slots("wsl", [128, 16, 128], 3)

    class Prefetch:
        def __init__(self, slots, aps, depth=None):
            self.t = slots.t
            self.aps = list(aps)
            self.n = len(self.t)
            self.depth = self.n - 1 if depth is None else depth
            self.i = 0
            self.issued = 0
            self._fill()

        def _fill(self):
            while self.issued < len(self.aps) and self.issued <= self.i + self.depth:
                t = self.t[self.issued % self.n]
                dma("pool", t[:], self.aps[self.issued], w=[t])
                self.issued += 1

        def next(self):
            self._fill()
            t = self.t[self.i % self.n]
            self.i += 1
            return t

    kT_seq = sb("kT_seq", [128, NT], BF16)
    vt_seq = sb("vt_seq", [128, 16, 128], BF16)
    kT_s = sb("kT_s", [128, 128 + TS], BF16)
    vt_s = sb("vt_s", [128, 2, 128], BF16)
    carry_p = sb("carry_p", [128, NRW])
    carry_s = sb("carry_s", [128, NRW])
    op("dve", lambda e: e.memset(carry_p[:], 0.0), w=[carry_p])
    dma("sp", carry_s[:], sshift_in, w=[carry_s])
    S32_p = sb("S32_p", [128, 8, 64])
    S32_s = sb("S32_s", [128, 8, 64])
    op("dve", lambda e: e.memset(S32_p[:], 0.0), w=[S32_p])
    dma("sp", S32_s[:], swkv_in.rearrange("h p i -> p h i"), w=[S32_s])
    cvc_p = sb("cvc_p", [128, NFT, 2])
    cvc_s = sb("cvc_s", [128, NFT, 2])
    op("dve", lambda e: e.memset(cvc_p[:], 0.0), w=[cvc_p])
    dma("sp", cvc_s[:], sconv_in, w=[cvc_s])
    op("dve", lambda e: e.tensor_scalar(omu[:], vec[:, V_MU:V_MU + NRW], -1.0, 1.0, ALU.mult, ALU.add), r=[vec], w=[omu])
    op("dve", lambda e: e.tensor_scalar(nka[:], vec[:, V_KA:V_KA + 8], -1.0, None, ALU.mult), r=[vec], w=[nka])
    with scope():
        ck = sb("ck", [128, 128])
        cv_ = sb("cv", [128, 128])
        ckb = sb("ckb", [128, 128], BF16)
        dma("sp", ck[:], cache_k, w=[ck])
        dma("sp", cv_[:], cache_v, w=[cv_])
        op("dve", lambda e: e.tensor_copy(ckb[:], ck[:]), r=[ck], w=[ckb])
        op("dve", lambda e: e.tensor_copy(vt_s[:, 0, :], cv_[:]), r=[cv_], w=[vt_s])
        pb = bank()
        op("pe", lambda e: e.matmul(pb[:, 0:128], ckb[:], identb, start=True, stop=True), r=[ckb, cb], w=[pb])
        op("act", lambda e: e.copy(kT_s[:, 0:128], pb[:, 0:128]), r=[pb], w=[kT_s])
        dma("sp", sk_o[0:96, :], ck[32:128, :], r=[ck], final=True)
        dma("sp", sv_o[0:96, :], cv_[32:128, :], r=[cv_], final=True)

    def ln_stats(x, rows, tag):
        uid[0] += 1
        st = sb("st_%s_%d" % (tag, uid[0]), [128, 24])
        mv = sb("mv_%s_%d" % (tag, uid[0]), [128, 4])
        for j in range(4):
            op("dve", lambda e, j=j: e.bn_stats(st[:rows, j * 6:(j + 1) * 6], x[:rows, j * 512:(j + 1) * 512]), r=[x], w=[st])
        op("dve", lambda e: e.bn_aggr(mv[:rows, 0:2], st[:rows, :]), r=[st], w=[mv])
        op("act", lambda e: e.activation(mv[:rows, 2:3], mv[:rows, 1:2], AF.Ln, bias=LN_EPS, scale=1.0), r=[mv], w=[mv])
        op("act", lambda e: e.activation(mv[:rows, 2:3], mv[:rows, 2:3], AF.Exp, scale=-0.5), r=[mv], w=[mv])
        op("dve", lambda e: e.scalar_tensor_tensor(mv[:rows, 3:4], mv[:rows, 0:1], -1.0, mv[:rows, 2:3], ALU.mult, ALU.mult), r=[mv], w=[mv])
        return mv

    def to_feature_major(xnb, rows, dstT, col0, gcol, bcol):
        for q4 in range(4):
            pb = bank()

            def mm(e, q4=q4, pb=pb):
                r_ = None
                for j in range(4):
                    kc = q4 * 4 + j
                    r_ = e.matmul(pb[:, j * 128:j * 128 + rows], xnb[:rows, kc * 128:(kc + 1) * 128], identb[:rows, :rows],
                                  start=True, stop=True)
                return r_
            op("pe", mm, r=[xnb, cb], w=[pb])
            for j in range(4):
                kc = q4 * 4 + j
                eng = "act" if q4 % 2 == 0 else "dve"
                if eng == "act":
                    op("act", lambda e, j=j, kc=kc, pb=pb: e.activation(
                        dstT[:, kc, col0:col0 + rows], pb[:, j * 128:j * 128 + rows], AF.Identity,
                        bias=vec[:, bcol + kc:bcol + kc + 1], scale=vec[:, gcol + kc:gcol + kc + 1]), r=[pb, vec], w=[dstT])
                else:
                    op("dve", lambda e, j=j, kc=kc, pb=pb: e.tensor_scalar(
                        dstT[:, kc, col0:col0 + rows], pb[:, j * 128:j * 128 + rows],
                        vec[:, gcol + kc:gcol + kc + 1], vec[:, bcol + kc:bcol + kc + 1], ALU.mult, ALU.add), r=[pb, vec], w=[dstT])

    for gi in range(ngroups):
        segs = [Seg("p", GT, 64, 0)]
        if gi == NG - 1:
            segs.append(Seg("s", TS, 32, GT))
        TG = sum(s.T for s in segs)
        full_post = gi >= 2
        with scope():
            hT = sb("hT", [128, 16, TG], BF16)
            mixT = sb("mixT", [128, 16, TG], BF16)
            with scope():
                xts = [sb("xt0", [128, D]), sb("xt1", [128, D])]
                xnbs = [sb("xnb0", [128, D], BF16), sb("xnb1", [128, D], BF16)]
                n = 0
                pend = [None]
                for sg in segs:
                    for ti, rows in enumerate(sg.rows):
                        xt = xts[n % 2]
                        xnb = xnbs[n % 2]
                        n += 1
                        src = xseq[gi * GT + ti * 128: gi * GT + ti * 128 + rows, :] if sg.name == "p" else xsmp[0:rows, :]
                        dma("sp", xt[:rows, :], src, w=[xt])
                        mv = ln_stats(xt, rows, "in%d" % (n % 2))
                        op("act", lambda e, xt=xt, xnb=xnb, mv=mv, rows=rows: e.activation(
                            xnb[:rows, :], xt[:rows, :], AF.Identity, bias=mv[:rows, 3:4], scale=mv[:rows, 2:3]), r=[xt, mv], w=[xnb])
                        fm = (lambda xnb=xnb, rows=rows, c0=sg.off + ti * 128: to_feature_major(xnb, rows, hT, c0, V_LNG, V_LNB))
                        if pend[0] is not None:
                            pend[0]()
                        pend[0] = fm
                if pend[0] is not None:
                    pend[0]()
            dump("hT%d" % gi, hT, hT[:], [128, 16, TG])
            if stop == "ln":
                continue
            with scope():
              if gi >= 1:
                    ntl = sum(s.ntile for s in segs)
                    qtok = sb("qtok", [128, ntl, 1024], BF16)
                    ktok = sb("ktok", [128, ntl, 128])
                    vtok = sb("vtok", [128, ntl, 128])
                    ktb = sb("ktb", [128, ntl, 128], BF16)
                    qT = sb("qT", [128, 8, TG], BF16)
                    cst_t = sb("cs_t", [128, ntl, 128])
                    n = 0
                    for sg in segs:
                        for ti, rows in enumerate(sg.rows):
                            src = cs_p[gi * GT + ti * 128: gi * GT + ti * 128 + rows, :] if sg.name == "p" else cs_s[0:rows, :]
                            dma("sp", cst_t[:rows, n, :], src, w=[cst_t])
                            n += 1
                    wab = Prefetch(Slots("wab", [128, 16, 256], 2), [w_att[b_] for b_ in range(5)])
                    tA = sb("ropeA", [128, 256])
                    tB = sb("ropeB", [128, 256])
                    for blk in range(5):
                        wt = wab.next()
                        n = 0
                        for sg in segs:
                            for ti, rows in enumerate(sg.rows):
                                if gi == 1 and ((blk < 4 and ti < 3) or (blk == 4 and ti < 2)):
                                    n += 1
                                    continue
                                pb = bank()
                                c0 = sg.off + ti * 128

                                def mm(e, pb=pb, wt=wt, c0=c0, rows=rows):
                                    r_ = None
                                    for kc in range(16):
                                        r_ = e.matmul(pb[:rows, 0:256], hT[:, kc, c0:c0 + rows], wt[:, kc, :], start=(kc == 0), stop=(kc == 15))
                                    return r_
                                op("pe", mm, r=[hT, wt], w=[pb])
                                if stop == "attproj1":
                                    op("act", lambda e, pb=pb, rows=rows: e.copy(tA[:rows, :], pb[:rows, 0:256]), r=[pb], w=[tA])
                                    n += 1
                                    continue
                                nh = 4 if blk < 4 else 2
                                w_ = nh * 64
                                cc = cst_t[:rows, n, 0:64].unsqueeze(1).to_broadcast([rows, nh, 64])
                                ss = cst_t[:rows, n, 64:128].unsqueeze(1).to_broadcast([rows, nh, 64])
                                x3 = pb[:rows, 0:w_].rearrange("p (h d) -> p h d", d=64)
                                A3 = tA[:rows, 0:w_].rearrange("p (h d) -> p h d", d=64)
                                B3 = tB[:rows, 0:w_].rearrange("p (h d) -> p h d", d=64)
                                op("dve", lambda e, A3=A3, x3=x3, cc=cc: e.tensor_tensor(A3, x3, cc, ALU.mult), r=[pb, cst_t], w=[tA])
                                op("dve", lambda e, B3=B3, x3=x3, ss=ss: e.tensor_tensor(B3, x3, ss, ALU.mult), r=[pb, cst_t], w=[tB])
                                if blk < 4:
                                    dst = qtok[:rows, n, blk * 256:(blk + 1) * 256].rearrange("p (h d) -> p h d", d=64)
                                    dT = qtok
                                else:
                                    dst = ktok[:rows, n, :].rearrange("p (h d) -> p h d", d=64)
                                    dT = ktok
                                op("dve", lambda e, dst=dst, A3=A3, B3=B3: e.tensor_tensor(dst[:, :, 0:32], A3[:, :, 0:32], B3[:, :, 32:64], ALU.subtract),
                                   r=[tA, tB], w=[dT])
                                op("dve", lambda e, dst=dst, A3=A3, B3=B3: e.tensor_tensor(dst[:, :, 32:64], A3[:, :, 32:64], B3[:, :, 0:32], ALU.add),
                                   r=[tA, tB], w=[dT])
                                if stop == "attproj2":
                                    n += 1
                                    continue
                                lvl = int(stop.split(":")[1]) if (stop and ":" in stop) else 99
                                if blk == 4:
                                    op("dve", lambda e, n=n, rows=rows, pb=pb: e.tensor_copy(vtok[:rows, n, :], pb[:rows, 128:256]), r=[pb], w=[vtok])
                                    if lvl < 1:
                                        n += 1
                                        continue
                                    op("act", lambda e, n=n, rows=rows: e.copy(ktb[:rows, n, :], ktok[:rows, n, :]), r=[ktok], w=[ktb])
                                    if lvl < 2:
                                        n += 1
                                        continue
                                    if sg.name == "p":
                                        gt = gi * 4 + ti
                                        op("dve", lambda e, n=n, gt=gt: e.tensor_copy(vt_seq[:, gt, :], vtok[:, n, :]), r=[vtok], w=[vt_seq])
                                    else:
                                        op("dve", lambda e, n=n, rows=rows: e.tensor_copy(vt_s[:rows, 1, :], vtok[:rows, n, :]), r=[vtok], w=[vt_s])
                                    if lvl < 3:
                                        n += 1
                                        continue
                                    pk = bank()
                                    op("pe", lambda e, pk=pk, n=n, rows=rows: e.matmul(pk[:, 0:rows], ktb[:rows, n, :], identb[:rows, :rows], start=True, stop=True),
                                       r=[ktb, cb], w=[pk])
                                    if sg.name == "p":
                                        g0 = gi * GT + ti * 128
                                        op("act", lambda e, pk=pk, g0=g0: e.copy(kT_seq[:, g0:g0 + 128], pk[:, 0:128]), r=[pk], w=[kT_seq])
                                    else:
                                        op("act", lambda e, pk=pk, rows=rows: e.copy(kT_s[:, 128:128 + rows], pk[:, 0:rows]), r=[pk], w=[kT_s])
                                n += 1
                    if gi == NG - 1:
                        dma("sp", pk_o, ktok[:, 3, :], r=[ktok], final=True)
                        dma("sp", pv_o, vtok[:, 3, :], r=[vtok], final=True)
                        dma("sp", sk_o[96:128, :], ktok[0:TS, 4, :], r=[ktok], final=True)
                        dma("sp", sv_o[96:128, :], vtok[0:TS, 4, :], r=[vtok], final=True)
                    if stop and stop.startswith("attproj"):
                        continue
                    n = 0
                    for sg in segs:
                        for ti, rows in enumerate(sg.rows):
                            c0 = sg.off + ti * 128
                            if gi == 1 and ti < 3:
                                n += 1
                                continue
                            for q4 in range(2):
                                pb = bank()

                                def mm(e, pb=pb, n=n, rows=rows, q4=q4):
                                    r_ = None
                                    for j in range(4):
                                        jj = q4 * 4 + j
                                        r_ = e.matmul(pb[:, j * 128:j * 128 + rows], qtok[:rows, n, jj * 128:(jj + 1) * 128], identb[:rows, :rows],
                                                      start=True, stop=True)
                                    return r_
                                op("pe", mm, r=[qtok, cb], w=[pb])
                                eng = "act" if q4 == 0 else "dve"
                                src = pb[:, :].rearrange("p (j t) -> p j t", t=128)[:, :, 0:rows]
                                dst = qT[:, q4 * 4:q4 * 4 + 4, c0:c0 + rows]
                                if eng == "act":
                                    op("act", lambda e, src=src, dst=dst: e.copy(dst, src), r=[pb], w=[qT])
                                else:
                                    op("dve", lambda e, src=src, dst=dst: e.tensor_copy(dst, src), r=[pb], w=[qT])
                            n += 1
                    dump("qT%d" % gi, qT, qT[:], [128, 8, TG])
                    if stop == "attproj":
                        continue
                    pT = [sb("pT0", [128, 2, 512], BF16), sb("pT1", [128, 2, 512], BF16)]
                    dsum = sb("dsum", [128, 256])
                    npt = 0
                    pending = [None]
                    for sg in segs:
                        Cq = sg.C
                        for c in range(sg.nch):
                            if gi == 1 and c < 6:
                                continue
                            qcols = slice(sg.off + c * Cq, sg.off + (c + 1) * Cq)
                            kb = []
                            if sg.name == "p":
                                cg = gi * 8 + c
                                m = cg // 2
                                if cg % 2 == 0:
                                    if m >= 1:
                                        kb.append((kT_seq, (m - 1) * 128, vt_seq, m - 1, 0, 128, cg in (16, 17)))
                                    kb.append((kT_seq, m * 128, vt_seq, m, 0, 64, False))
                                else:
                                    if m >= 1:
                                        kb.append((kT_seq, (m - 1) * 128 + 64, vt_seq, m - 1, 64, 64, cg in (16, 17)))
                                    kb.append((kT_seq, m * 128, vt_seq, m, 0, 128, False))
                            else:
                                kb.append((kT_s, 0, vt_s, 0, 0, 128, False))
                                kb.append((kT_s, 128, vt_s, 1, 0, TS, False))
                            for g in range(2):
                                N = 8 * Cq
                                pt = pT[npt % 2]
                                npt += 1
                                for bi, (kt, kc0, vt, vti, pr0, nk, uf) in enumerate(kb):
                                    ps_ = bank()
                                    rhs = qT[g * 64:(g + 1) * 64, :, qcols]
                                    op("pe", lambda e, ps_=ps_, kt=kt, kc0=kc0, nk=nk, rhs=rhs, pr0=pr0, g=g, N=N, Cq=Cq: e.matmul(
                                        ps_[pr0:pr0 + nk, 0:N].rearrange("p (h q) -> p h q", q=Cq), kt[g * 64:(g + 1) * 64, kc0:kc0 + nk], rhs,
                                        start=True, stop=True, tile_position=(g * 64, pr0)), r=[kt, qT], w=[ps_])
                                    op("act", lambda e, pt=pt, bi=bi, ps_=ps_, pr0=pr0, nk=nk, N=N: e.activation(
                                        pt[pr0:pr0 + nk, bi, 0:N], ps_[pr0:pr0 + nk, 0:N], AF.Exp, scale=ATT_SCALE), r=[ps_], w=[pt])
                                    if uf:
                                        op("dve", lambda e, pt=pt, bi=bi, pr0=pr0, nk=nk, N=N: e.tensor_scalar(
                                            pt[pr0:pr0 + nk, bi, 0:N], pt[pr0:pr0 + nk, bi, 0:N], flag[pr0:pr0 + nk, 0:1], None, ALU.mult),
                                           r=[pt, flag], w=[pt])
                                def stage2(pt=pt, kb=kb, g=g, Cq=Cq, qcols=qcols):
                                    po = bank()
                                    pd = bank()
                                    Nh = 4 * Cq

                                    def mmo(e, po=po, pd=pd, pt=pt, kb=kb, g=g, Nh=Nh):
                                        r_ = None
                                        for hh in range(2):
                                            for bi, (kt, kc0, vt, vti, pr0, nk, uf) in enumerate(kb):
                                                e.matmul(po[hh * 64:(hh + 1) * 64, 0:Nh], vt[pr0:pr0 + nk, vti, g * 64:(g + 1) * 64],
                                                         pt[pr0:pr0 + nk, bi, hh * Nh:(hh + 1) * Nh], start=(bi == 0), stop=(bi == len(kb) - 1),
                                                         tile_position=(pr0, hh * 64))
                                        for hh in range(2):
                                            for bi, (kt, kc0, vt, vti, pr0, nk, uf) in enumerate(kb):
                                                r_ = e.matmul(pd[hh * 64:(hh + 1) * 64, 0:Nh], onesb[pr0:pr0 + nk, 0:64],
                                                              pt[pr0:pr0 + nk, bi, hh * Nh:(hh + 1) * Nh], start=(bi == 0), stop=(bi == len(kb) - 1),
                                                              tile_position=(pr0, hh * 64))
                                        return r_
                                    vts = list({id(k[2]): k[2] for k in kb}.values())
                                    op("pe", mmo, r=[pt, cb] + vts, w=[po, pd])
                                    op("dve", lambda e, pd=pd, g=g, Nh=Nh, Cq=Cq: e.tensor_tensor(
                                        dsum[:, 0:Nh].rearrange("p (i q) -> p i q", q=Cq), pd[:, 0:Nh].rearrange("p (i q) -> p i q", q=Cq),
                                        esb[:, g, :, 0:Cq], ALU.add), r=[pd, esb], w=[dsum])
                                    op("dve", lambda e, Nh=Nh: e.reciprocal(dsum[:, 0:Nh], dsum[:, 0:Nh]), r=[dsum], w=[dsum])
                                    op("dve", lambda e, po=po, g=g, Nh=Nh, Cq=Cq, qcols=qcols: e.tensor_tensor(
                                        mixT[:, g * 4:g * 4 + 4, qcols], po[:, 0:Nh].rearrange("p (i q) -> p i q", q=Cq),
                                        dsum[:, 0:Nh].rearrange("p (i q) -> p i q", q=Cq), ALU.mult), r=[po, dsum], w=[mixT])
                                if pending[0] is not None:
                                    pending[0]()
                                pending[0] = stage2
                    if pending[0] is not None:
                        pending[0]()
            dump("att%d" % gi, mixT, mixT[:, 0:8, :], [128, 8, TG])
            if stop == "att":
                continue
            if gi == 2:
                op("dve", lambda e: e.tensor_scalar(carry_p[:], carry_p[:], flag[:, 0:1], None, ALU.mult), r=[carry_p, flag], w=[carry_p])
                op("dve", lambda e: e.tensor_scalar(S32_p[:], S32_p[:], flag[:, 0:1], None, ALU.mult), r=[S32_p, flag], w=[S32_p])
            rsegs = []
            for sg in segs:
                if sg.name == "p":
                    for hf in range(2):
                        r_ = Seg("p%d" % hf, GT // 2, 64, hf * (GT // 2))
                        r_.carry, r_.S32 = carry_p, S32_p
                        rsegs.append(r_)
                else:
                    r_ = Seg("s", TS, 32, sg.off)
                    r_.carry, r_.S32 = carry_s, S32_s
                    rsegs.append(r_)

            for r_ in rsegs:
                r_.out = (gi >= 2) or (gi == 1 and r_.name == "p1") or r_.name == "s"
            need_r = any(r_.out for r_ in rsegs)

            def rr(gens):
                gens = list(gens)
                while gens:
                    for g_ in list(gens):
                        try:
                            next(g_)
                        except StopIteration:
                            gens.remove(g_)

            def v3(ap, C):
                return ap.rearrange("p (c t) -> p c t", t=C)

            with scope():
                lor = {sg.name: sb("lor_" + sg.name, [128, 3, sg.T], BF16) for sg in rsegs}
                raws = [{sg.name: sb("raw%d_%s" % (i, sg.name), [128, sg.T + 1]) for sg in rsegs} for i in range(2)]
                t1s = [{sg.name: sb("sht%d_%s" % (i, sg.name), [128, sg.T]) for sg in rsegs} for i in range(2)]
                rawi = [0]
                rw_order = [0, 1, 2] + [3 + hp_ * 3 + j_ for hp_ in range(8) for j_ in range(3) if (need_r or j_ != 0)]
                rwq = Prefetch(wsl, [w_rw[t_] for t_ in rw_order])
                rw_pos = [0]

                def rw_inproj(tidx, dst):
                    assert rw_order[rw_pos[0]] == tidx
                    rw_pos[0] += 1
                    wt = rwq.next()
                    k_ = rawi[0] % 2
                    rawi[0] += 1
                    pbs = {}
                    for sg0 in segs:
                        pb = bank()
                        T0_ = sg0.T

                        def mm(e, pb=pb, wt=wt, sg0=sg0, T0_=T0_):
                            r_ = None
                            for kc in range(16):
                                r_ = e.matmul(pb[:, 0:T0_], wt[:, kc, :], hT[:, kc, sg0.off:sg0.off + T0_], start=(kc == 0), stop=(kc == 15))
                            return r_
                        op("pe", mm, r=[hT, wt], w=[pb])
                        for sg in rsegs:
                            if sg.off >= sg0.off and sg.off < sg0.off + T0_:
                                pbs[sg.name] = (pb, sg.off - sg0.off)
                    for sg in rsegs:
                        pb, po_ = pbs[sg.name]
                        T_ = sg.T
                        raw = raws[k_][sg.name]
                        t1 = t1s[k_][sg.name]
                        d_ = dst[sg.name]
                        op("act", lambda e: e.copy(raw[:, 1:1 + T_], pb[:, po_:po_ + T_]), r=[pb], w=[raw])
                        op("dve", lambda e: e.tensor_copy(raw[:, 0:1], sg.carry[:, tidx:tidx + 1]), r=[sg.carry], w=[raw])
                        op("dve", lambda e: e.tensor_copy(sg.carry[:, tidx:tidx + 1], raw[:, T_:T_ + 1]), r=[raw], w=[sg.carry])
                        op("act", lambda e: e.activation(t1[:, 0:T_], raw[:, 0:T_], AF.Identity, scale=vec[:, V_MU + tidx:V_MU + tidx + 1]), r=[raw, vec], w=[t1])
                        op("dve", lambda e: e.scalar_tensor_tensor(d_[:, 0:T_], raw[:, 1:1 + T_], omu[:, tidx:tidx + 1], t1[:, 0:T_], ALU.mult, ALU.add),
                           r=[raw, t1, omu], w=[d_])

                lt = {sg.name: sb("lorf_" + sg.name, [128, sg.T]) for sg in rsegs}
                rw_inproj(0, lt)
                for sg in rsegs:
                    op("act", lambda e: e.activation(lor[sg.name][0:64, 0, :], lt[sg.name][0:64, :], AF.Tanh), r=[lt[sg.name]], w=[lor[sg.name]])
                    op("dve", lambda e: e.tensor_copy(lor[sg.name][64:128, 0, :], lt[sg.name][64:128, :]), r=[lt[sg.name]], w=[lor[sg.name]])
                rw_inproj(1, lt)
                for sg in rsegs:
                    op("act", lambda e: e.activation(lor[sg.name][:, 1, :], lt[sg.name][:], AF.Sigmoid), r=[lt[sg.name]], w=[lor[sg.name]])
                rw_inproj(2, lt)
                for sg in rsegs:
                    op("act", lambda e: e.activation(lor[sg.name][0:32, 2, :], lt[sg.name][0:32, :], AF.Sigmoid), r=[lt[sg.name]], w=[lor[sg.name]])

                I2b = cb[:, 320:384]
                HS = 4
                TMPN = ("xr", "xk", "xv", "sig", "cum", "av", "Pt", "iP", "Pp", "Eh", "tmp", "tmp2", "tmp3", "kkt", "kmod", "bvec")
                for hset in range(8 // HS):
                    with scope():
                        hps = list(range(hset * HS, (hset + 1) * HS))
                        hsl = slice(hset * HS, (hset + 1) * HS)

                        def per(nm, shape_fn, dt=BF16):
                            return [{sg.name: sb("%s%d_%s" % (nm, i, sg.name), shape_fn(sg), dt) for sg in rsegs} for i in range(HS)]
                        AR = per("AR", lambda sg: [128, sg.nch, 2, 64])
                        W4 = per("W4", lambda sg: [128, sg.nch, 2, 64])
                        K4 = per("K4", lambda sg: [128, sg.nch, 2, 64])
                        Tt = per("Tt", lambda sg: [128, sg.nch, 64])
                        VT = per("VT", lambda sg: [128, sg.nch, 3, 64])
                        gv = per("gv", lambda sg: [128, sg.T])
                        bon = per("bon", lambda sg: [128, sg.T])
                        PC = {sg.name: sb("PC_" + sg.name, [128, sg.nch, HS]) for sg in rsegs}
                        Osb = {sg.name: sb("Osb_" + sg.name, [128, HS, sg.T]) for sg in rsegs}
                        with scope():
                            tm = {sg.name: {nm: sb("%s_%s" % (nm, sg.name), [128, sg.T]) for nm in TMPN} for sg in rsegs}
                            xb2 = [tm, tm]
                            BK2 = [{sg.name: sb("BK%d_%s" % (i, sg.name), [128, sg.nch, 2, 64], BF16) for sg in rsegs} for i in range(2)]
                            HAT2 = [{sg.name: sb("HAT%d_%s" % (i, sg.name), [128, sg.nch, 3, 64], BF16) for sg in rsegs} for i in range(2)]
                            Lc2 = [[{sg.name: sb("Lc%d%d_%s" % (j, i, sg.name), [128, sg.nch, 64], BF16) for sg in rsegs} for i in range(2)] for j in range(2)]
                            Wc2 = [[{sg.name: sb("Wc%d%d_%s" % (j, i, sg.name), [128, sg.nch, 64], BF16) for sg in rsegs} for i in range(2)] for j in range(2)]

                            def prep(hi, hp, sg, part):
                                T_, C, nch = sg.T, sg.C, sg.nch
                                n_ = sg.name
                                t = tm[n_]
                                BK, HAT, Lc, Wc = BK2[hi % 2], HAT2[hi % 2], Lc2[hi % 2], Wc2[hi % 2]
                                xs_ = xb2[hi % 2][n_]
                                xr, xk, xv, sig, cum, av = xs_["xr"], xs_["xk"], xs_["xv"], t["sig"], t["cum"], t["av"]
                                Pt, iP, Pp, Eh, tmp, tmp2, tmp3 = t["Pt"], t["iP"], t["Pp"], t["Eh"], t["tmp"], t["tmp2"], t["tmp3"]
                                kkt, kmod, bvec = t["kkt"], t["kmod"], t["bvec"]
                                lr = lor[n_]
                                ARh, W4h, K4h, Tth, VTh, BKs, HATs = AR[hi][n_], W4[hi][n_], K4[hi][n_], Tt[hi][n_], VT[hi][n_], BK[n_], HAT[n_]
                                hc = slice(hp * 128, (hp + 1) * 128)
                                if part == "B":
                                    yield from prepB(hi, hp, sg, ARh, W4h, K4h, Tth, VTh, BKs, HATs, Lc, Wc)
                                    return
                                pw = bank()
                                op("pe", lambda e: e.matmul(pw[:, 0:T_], lwa[0:64, hc], lr[0:64, 0, :], start=True, stop=True, tile_position=(0, 0)), r=[lwa, lr], w=[pw])
                                op("act", lambda e: e.activation(sig[:], pw[:, 0:T_], AF.Sigmoid, bias=vec[:, V_W0 + hp:V_W0 + hp + 1]), r=[pw, vec], w=[sig])
                                yield
                                pa = bank()
                                op("pe", lambda e: e.matmul(pa[:, 0:T_], lwa[64:128, hc], lr[64:128, 0, :], start=True, stop=True, tile_position=(64, 0)), r=[lwa, lr], w=[pa])
                                op("act", lambda e: e.activation(av[:], pa[:, 0:T_], AF.Sigmoid, bias=vec[:, V_A0 + hp:V_A0 + hp + 1]), r=[pa, vec], w=[av])
                                yield
                                if sg.out:
                                    pg = bank()

                                    def mmg(e):
                                        e.matmul(pg[:, 0:T_], lg[:, hc], lr[:, 1, :], start=True, stop=False)
                                        return e.matmul(pg[:, 0:T_], lgb[0:32, hc], lr[0:32, 2, :], start=False, stop=True)
                                    op("pe", mmg, r=[lg, lgb, lr], w=[pg])
                                    op("act", lambda e: e.copy(gv[hi][n_][:], pg[:, 0:T_]), r=[pg], w=[gv[hi][n_]])
                                yield
                                op("dve", lambda e: e.tensor_tensor_scan(cum[:], rmask[:, 0:T_], sig[:], 0.0, ALU.mult, ALU.add), r=[sig, cst], w=[cum])
                                yield
                                op("act", lambda e: e.activation(Pt[:], cum[:], AF.Exp, scale=-C0), r=[cum], w=[Pt])
                                op("act", lambda e: e.activation(iP[:], cum[:], AF.Exp, scale=C0), r=[cum], w=[iP])
                                op("pool", lambda e: e.tensor_tensor(tmp[:], cum[:], sig[:], ALU.subtract), r=[cum, sig], w=[tmp])
                                yield
                                op("act", lambda e: e.activation(Pp[:], tmp[:], AF.Exp, scale=-C0), r=[tmp], w=[Pp])
                                cum3 = v3(cum[:], C)
                                op("dve", lambda e: e.tensor_tensor(v3(tmp2[:], C), cum3, cum3[:, :, C - 1:C].to_broadcast([128, nch, C]), ALU.subtract), r=[cum], w=[tmp2])
                                yield
                                op("act", lambda e: e.activation(Eh[:], tmp2[:], AF.Exp, scale=C0), r=[tmp2], w=[Eh])
                                op("act", lambda e: e.activation(PC[n_][:, :, hi:hi + 1], cum3[:, :, C - 1:C], AF.Exp, scale=-C0), r=[cum], w=[PC[n_]])
                                op("act", lambda e: e.activation(tmp3[:], xk[:], AF.Square, scale=vec[:, V_KK + hp:V_KK + hp + 1]), r=[xk, vec], w=[tmp3])
                                yield
                                pn = bank()
                                op("pe", lambda e: e.matmul(pn[:, 0:T_], bones, tmp3[:], start=True, stop=True), r=[cst, tmp3], w=[pn])
                                op("act", lambda e: e.activation(tmp3[:], pn[:, 0:T_], AF.Ln, bias=1e-18, scale=1.0), r=[pn], w=[tmp3])
                                yield
                                op("act", lambda e: e.activation(tmp3[:], tmp3[:], AF.Exp, scale=-0.5), r=[tmp3], w=[tmp3])
                                op("act", lambda e: e.activation(kkt[:], xk[:], AF.Identity, scale=vec[:, V_KK + hp:V_KK + hp + 1]), r=[xk, vec], w=[kkt])
                                op("pool", lambda e: e.tensor_tensor(kkt[:], kkt[:], tmp3[:], ALU.mult), r=[kkt, tmp3], w=[kkt])
                                yield
                                op("act", lambda e: e.activation(tmp[:], av[:], AF.Identity, scale=vec[:, V_KA + hp:V_KA + hp + 1], bias=nka[:, hp:hp + 1]), r=[av, vec, nka], w=[tmp])
                                op("dve", lambda e: e.scalar_tensor_tensor(kmod[:], tmp[:], 1.0, xk[:], ALU.add, ALU.mult), r=[tmp, xk], w=[kmod])
                                yield
                                op("pool", lambda e: e.tensor_tensor(bvec[:], kkt[:], av[:], ALU.mult), r=[kkt, av], w=[bvec])
                                op("dve", lambda e: e.scalar_tensor_tensor(ARh[:, :, 0, 0:C], v3(kkt[:], C), -1.0, v3(Pp[:], C), ALU.mult, ALU.mult), r=[kkt, Pp], w=[ARh])
                                yield
                                if sg.out:
                                    op("pool", lambda e: e.tensor_tensor(ARh[:, :, 1, 0:C], v3(xr[:], C), v3(Pt[:], C), ALU.mult), r=[xr, Pt], w=[ARh])
                                op("pool", lambda e: e.tensor_tensor(BKs[:, :, 0, 0:C], v3(bvec[:], C), v3(iP[:], C), ALU.mult), r=[bvec, iP], w=[BKs])
                                yield
                                op("pool", lambda e: e.tensor_tensor(BKs[:, :, 1, 0:C], v3(kmod[:], C), v3(iP[:], C), ALU.mult), r=[kmod, iP], w=[BKs])
                                op("pool", lambda e: e.tensor_tensor(HATs[:, :, 0, 0:C], v3(bvec[:], C), v3(Eh[:], C), ALU.mult), r=[bvec, Eh], w=[HATs])
                                yield
                                op("pool", lambda e: e.tensor_tensor(HATs[:, :, 1, 0:C], v3(kmod[:], C), v3(Eh[:], C), ALU.mult), r=[kmod, Eh], w=[HATs])
                                op("act", lambda e: e.copy(HATs[:, :, 2, 0:C], v3(xv[:], C)), r=[xv], w=[HATs])
                                if sg.out:
                                    op("dve", lambda e: e.scalar_tensor_tensor(tmp[:], xr[:], vec[:, V_RK + hp:V_RK + hp + 1], kmod[:], ALU.mult, ALU.mult), r=[xr, vec, kmod], w=[tmp])
                                    yield
                                    pbn = bank()
                                    op("pe", lambda e: e.matmul(pbn[:, 0:T_], bones, tmp[:], start=True, stop=True), r=[cst, tmp], w=[pbn])
                                    op("dve", lambda e: e.tensor_tensor(bon[hi][n_][:], pbn[:, 0:T_], xv[:], ALU.mult), r=[pbn, xv], w=[bon[hi][n_]])
                                yield
                                return

                            def prepB(hi, hp, sg, ARh, W4h, K4h, Tth, VTh, BKs, HATs, Lc, Wc):
                                T_, C, nch = sg.T, sg.C, sg.nch
                                n_ = sg.name
                                pL = bank()

                                def mmL(e):
                                    r_ = None
                                    for c in range(nch):
                                        for e_ in range(2):
                                            ps_ = slice(e_ * 64, e_ * 64 + 64)
                                            r_ = e.matmul(pL[e_ * 64:e_ * 64 + C, c * 64:c * 64 + C], ARh[ps_, c, 0, 0:C], BKs[ps_, c, 0, 0:C],
                                                          start=True, stop=True, tile_position=(e_ * 64, e_ * 64))
                                    return r_
                                op("pe", mmL, r=[ARh, BKs], w=[pL])
                                L0 = Lc[0][n_]
                                W0 = Wc[0][n_]
                                op("dve", lambda e: e.tensor_tensor(L0[:, :, 0:C], v3(pL[:, 0:nch * 64], 64)[:, :, 0:C],
                                                                     mL[:, 0:C].unsqueeze(1).to_broadcast([128, nch, C]), ALU.mult), r=[pL, cst], w=[L0])
                                yield
                                mk = mUI.rearrange("p (x t) -> p x t", t=64)
                                for (srcBK, dst4) in ((0, W4h), (1, K4h)):
                                    pW = bank()

                                    def mmW(e, pW=pW, srcBK=srcBK):
                                        r_ = None
                                        for c in range(nch):
                                            for e_ in range(2):
                                                ps_ = slice(e_ * 64, e_ * 64 + 64)
                                                if sg.out:
                                                    r_ = e.matmul(pW[e_ * 64:e_ * 64 + C, c * 128:c * 128 + 2 * C].rearrange("p (x t) -> p x t", t=C),
                                                                  BKs[ps_, c, srcBK, 0:C], ARh[ps_, c, :, 0:C], start=True, stop=True, tile_position=(e_ * 64, e_ * 64))
                                                else:
                                                    r_ = e.matmul(pW[e_ * 64:e_ * 64 + C, c * 128:c * 128 + C],
                                                                  BKs[ps_, c, srcBK, 0:C], ARh[ps_, c, 0, 0:C], start=True, stop=True, tile_position=(e_ * 64, e_ * 64))
                                        return r_
                                    op("pe", mmW, r=[ARh, BKs], w=[pW])
                                    for x in range(2 if sg.out else 1):
                                        if C == 64:
                                            src = pW[:, 0:nch * 128].rearrange("p (c x t) -> p c x t", x=2, t=64)[:, :, x, :]
                                        else:
                                            src = pW[:, 0:2 * C].rearrange("p (c x t) -> p c x t", c=1, x=2, t=C)[:, :, x, :]
                                        op("dve", lambda e, src=src, x=x, dst4=dst4: e.tensor_tensor(
                                            dst4[:, :, x, 0:C], src, mk[:, x, 0:C].unsqueeze(1).to_broadcast([128, nch, C]), ALU.mult), r=[pW, cst], w=[dst4])
                                    yield
                                op("dve", lambda e: e.tensor_tensor(Tth[:, :, 0:C], W4h[:, :, 0, 0:C], I2b[:, 0:C].unsqueeze(1).to_broadcast([128, nch, C]), ALU.add),
                                   r=[W4h, cb], w=[Tth])
                                op("act", lambda e: e.copy(W0[:, :, 0:C], W4h[:, :, 0, 0:C]), r=[W4h], w=[W0])
                                yield
                                nlev = 5 if C == 64 else 4

                                def mmsq(e, pX, A, B):
                                    r_ = None
                                    for c in range(nch):
                                        for e_ in range(2):
                                            rs = slice(e_ * 64, e_ * 64 + C)
                                            r_ = e.matmul(pX[rs, c * 64:c * 64 + C], A[rs, c, 0:C], B[rs, c, 0:C], start=True, stop=True,
                                                          tile_position=(e_ * 64, e_ * 64))
                                    return r_
                                for lv in range(1, nlev + 1):
                                    Lo, Wo = Lc[(lv - 1) % 2][n_], Wc[(lv - 1) % 2][n_]
                                    Ln_, Wn = Lc[lv % 2][n_], Wc[lv % 2][n_]
                                    pL2 = bank()
                                    op("pe", lambda e, pL2=pL2, Lo=Lo, Wo=Wo: mmsq(e, pL2, Wo, Lo), r=[Wo, Lo], w=[pL2])
                                    if lv % 2 == 1:
                                        op("act", lambda e, pL2=pL2, Ln_=Ln_: e.copy(Ln_[:, :, 0:C], v3(pL2[:, 0:nch * 64], 64)[:, :, 0:C]), r=[pL2], w=[Ln_])
                                    else:
                                        op("dve", lambda e, pL2=pL2, Ln_=Ln_: e.tensor_copy(Ln_[:, :, 0:C], v3(pL2[:, 0:nch * 64], 64)[:, :, 0:C]), r=[pL2], w=[Ln_])
                                    if lv < nlev:
                                        pW2 = bank()
                                        op("pe", lambda e, pW2=pW2, Lo=Lo, Wo=Wo: mmsq(e, pW2, Lo, Wo), r=[Wo, Lo], w=[pW2])
                                        if lv % 2 == 1:
                                            op("dve", lambda e, pW2=pW2, Wn=Wn: e.tensor_copy(Wn[:, :, 0:C], v3(pW2[:, 0:nch * 64], 64)[:, :, 0:C]), r=[pW2], w=[Wn])
                                        else:
                                            op("act", lambda e, pW2=pW2, Wn=Wn: e.copy(Wn[:, :, 0:C], v3(pW2[:, 0:nch * 64], 64)[:, :, 0:C]), r=[pW2], w=[Wn])
                                    yield
                                    pT_ = bank()
                                    op("pe", lambda e, pT_=pT_, Ln_=Ln_: mmsq(e, pT_, Ln_, Tth), r=[Ln_, Tth], w=[pT_])
                                    op("dve", lambda e, pT_=pT_: e.tensor_tensor(Tth[:, :, 0:C], v3(pT_[:, 0:nch * 64], 64)[:, :, 0:C], Tth[:, :, 0:C], ALU.add),
                                       r=[pT_, Tth], w=[Tth])
                                    yield
                                for x in range(3):
                                    pV = bank()

                                    def mmV(e, pV=pV, x=x):
                                        r_ = None
                                        for c in range(nch):
                                            for e_ in range(2):
                                                ps_ = slice(e_ * 64, e_ * 64 + 64)
                                                r_ = e.matmul(pV[e_ * 64:e_ * 64 + C, c * 64:c * 64 + 64], HATs[ps_, c, x, 0:C], identb[ps_, e_ * 64:e_ * 64 + 64],
                                                              start=True, stop=True, tile_position=(e_ * 64, e_ * 64))
                                        return r_
                                    op("pe", mmV, r=[HATs, cb], w=[pV])
                                    if x == 1:
                                        op("act", lambda e, pV=pV, x=x: e.copy(VTh[:, :, x, :], v3(pV[:, 0:nch * 64], 64)), r=[pV], w=[VTh])
                                    else:
                                        op("act", lambda e, pV=pV, x=x: e.copy(VTh[:, :, x, :], v3(pV[:, 0:nch * 64], 64)), r=[pV], w=[VTh])
                                    yield

                            def inproj3(hi, hp):
                                xx = xb2[hi % 2]
                                if need_r:
                                    rw_inproj(3 + hp * 3 + 0, {n_: xx[n_]["xr"] for n_ in xx})
                                rw_inproj(3 + hp * 3 + 1, {n_: xx[n_]["xk"] for n_ in xx})
                                rw_inproj(3 + hp * 3 + 2, {n_: xx[n_]["xv"] for n_ in xx})
                            def stageA(hi, hp):
                                inproj3(hi, hp)
                                yield
                                gens = [prep(hi, hp, sg, "A") for sg in rsegs]
                                while gens:
                                    for g_ in list(gens):
                                        try:
                                            next(g_)
                                        except StopIteration:
                                            gens.remove(g_)
                                    yield
                            rr([stageA(0, hps[0])])
                            for hi, hp in enumerate(hps):
                                gl = [prep(hi, hp, sg, "B") for sg in rsegs]
                                if hi + 1 < HS:
                                    gl = [stageA(hi + 1, hps[hi + 1])] + gl
                                rr(gl)
                        Sb = [sb("Sb0", [128, HS, 64], BF16), sb("Sb1", [128, HS, 64], BF16)]
                        Xb = sb("Xb", [128, HS, 64], BF16)
                        Ub = sb("Ub", [128, HS, 64], BF16)
                        for sg in rsegs:
                            C, nch, n_ = sg.C, sg.nch, sg.name
                            S32 = sg.S32
                            par = 0
                            ARs = [AR[hi][n_] for hi in range(HS)]
                            W4s = [W4[hi][n_] for hi in range(HS)]
                            K4s = [K4[hi][n_] for hi in range(HS)]
                            Tts = [Tt[hi][n_] for hi in range(HS)]
                            VTs = [VT[hi][n_] for hi in range(HS)]
                            op("act", lambda e: e.copy(Sb[0][:], S32[:, hsl, :]), r=[S32], w=[Sb[0]])
                            for c in range(nch):
                                Sc, Sn = Sb[par], Sb[1 - par]
                                par = 1 - par
                                pX = bank()

                                def mmX(e):
                                    r_ = None
                                    for hi in range(HS):
                                        for e_ in range(2):
                                            ps_ = slice(e_ * 64, e_ * 64 + 64)
                                            rs = slice(e_ * 64, e_ * 64 + C)
                                            e.matmul(pX[rs, hi * 64:hi * 64 + 64], ARs[hi][ps_, c, 0, 0:C], Sc[ps_, hi, :], start=True, stop=False,
                                                     tile_position=(e_ * 64, e_ * 64))
                                            r_ = e.matmul(pX[rs, hi * 64:hi * 64 + 64], K4s[hi][rs, c, 0, 0:C], VTs[hi][rs, c, 2, :], start=False, stop=True,
                                                          tile_position=(e_ * 64, e_ * 64))
                                    return r_
                                op("pe", mmX, r=ARs + K4s + VTs + [Sc], w=[pX])
                                op("act", lambda e: e.copy(Xb[:].rearrange("p h i -> p (h i)"), pX[:, 0:HS * 64]), r=[pX], w=[Xb])
                                op("dve", lambda e: e.tensor_tensor(S32[:, hsl, :], S32[:, hsl, :], PC[n_][:, c, :].unsqueeze(2).to_broadcast([128, HS, 64]), ALU.mult),
                                   r=[S32, PC[n_]], w=[S32])
                                pU = bank()

                                def mmU(e):
                                    r_ = None
                                    for hi in range(HS):
                                        for e_ in range(2):
                                            rs = slice(e_ * 64, e_ * 64 + C)
                                            r_ = e.matmul(pU[rs, hi * 64:hi * 64 + 64], Tts[hi][rs, c, 0:C], Xb[rs, hi, :], start=True, stop=True,
                                                          tile_position=(e_ * 64, e_ * 64))
                                    return r_
                                op("pe", mmU, r=Tts + [Xb], w=[pU])
                                op("dve", lambda e: e.tensor_copy(Ub[:].rearrange("p h i -> p (h i)"), pU[:, 0:HS * 64]), r=[pU], w=[Ub])
                                pS = bank()

                                def mmS(e):
                                    r_ = None
                                    for hi in range(HS):
                                        for e_ in range(2):
                                            ps_ = slice(e_ * 64, e_ * 64 + 64)
                                            rs = slice(e_ * 64, e_ * 64 + C)
                                            e.matmul(pS[ps_, hi * 64:hi * 64 + 64], VTs[hi][rs, c, 0, :], Ub[rs, hi, :], start=True, stop=False,
                                                     tile_position=(e_ * 64, e_ * 64))
                                            r_ = e.matmul(pS[ps_, hi * 64:hi * 64 + 64], VTs[hi][rs, c, 1, :], VTs[hi][rs, c, 2, :], start=False, stop=True,
                                                          tile_position=(e_ * 64, e_ * 64))
                                    return r_
                                op("pe", mmS, r=VTs + [Ub], w=[pS])
                                op("dve", lambda e: e.tensor_tensor(S32[:, hsl, :], v3(pS[:, 0:HS * 64], 64), S32[:, hsl, :], ALU.add), r=[S32, pS], w=[S32])
                                if c < nch - 1:
                                    op("act", lambda e: e.copy(Sn[:], S32[:, hsl, :]), r=[S32], w=[Sn])
                                pO = bank()

                                def mmO(e):
                                    r_ = None
                                    for hi in range(HS):
                                        for e_ in range(2):
                                            ps_ = slice(e_ * 64, e_ * 64 + 64)
                                            rs = slice(e_ * 64, e_ * 64 + C)
                                            e.matmul(pO[ps_, hi * 64:hi * 64 + C], Sc[ps_, hi, :], ARs[hi][ps_, c, 1, 0:C], start=True, stop=False,
                                                     tile_position=(e_ * 64, e_ * 64))
                                            e.matmul(pO[ps_, hi * 64:hi * 64 + C], Ub[rs, hi, :], W4s[hi][rs, c, 1, 0:C], start=False, stop=False,
                                                     tile_position=(e_ * 64, e_ * 64))
                                            r_ = e.matmul(pO[ps_, hi * 64:hi * 64 + C], VTs[hi][rs, c, 2, :], K4s[hi][rs, c, 1, 0:C], start=False, stop=True,
                                                          tile_position=(e_ * 64, e_ * 64))
                                    return r_
                                if sg.out:
                                    op("pe", mmO, r=ARs + W4s + K4s + VTs + [Sc, Ub], w=[pO])
                                    op("act", lambda e: e.copy(Osb[n_][:, :, c * C:(c + 1) * C], v3(pO[:, 0:HS * 64], 64)[:, :, 0:C]), r=[pO], w=[Osb[n_]])
                        with scope():
                            ptm = [{sg.name: {nm: sb("%s%d_%s" % (nm, i, sg.name), [128, sg.T]) for nm in ("osq", "msb", "tq")} for sg in rsegs} for i in range(HS)]

                            def post(hi, hp, sg):
                                T_, n_ = sg.T, sg.name
                                osq, msb, tq = ptm[hi][n_]["osq"], ptm[hi][n_]["msb"], ptm[hi][n_]["tq"]
                                O_ = Osb[n_]
                                cs = slice(sg.off, sg.off + T_)
                                pm = bank()
                                op("pe", lambda e: e.matmul(pm[:, 0:T_], bones64[:], O_[:, hi, :], start=True, stop=True), r=[bones64, O_], w=[pm])
                                op("act", lambda e: e.copy(msb[:], pm[:, 0:T_]), r=[pm], w=[msb])
                                yield
                                op("act", lambda e: e.activation(osq[:], O_[:, hi, :], AF.Square), r=[O_], w=[osq])
                                op("dve", lambda e: e.tensor_tensor(tq[:], msb[:], msb[:], ALU.mult), r=[msb], w=[tq])
                                yield
                                pq = bank()
                                op("pe", lambda e: e.matmul(pq[:, 0:T_], bones64[:], osq[:], start=True, stop=True), r=[bones64, osq], w=[pq])
                                op("dve", lambda e: e.tensor_tensor(tq[:], pq[:, 0:T_], tq[:], ALU.subtract), r=[pq, tq], w=[tq])
                                yield
                                op("act", lambda e: e.activation(tq[:], tq[:], AF.Ln, bias=LNX_EPS, scale=1.0), r=[tq], w=[tq])
                                op("pool", lambda e: e.tensor_tensor(osq[:], O_[:, hi, :], msb[:], ALU.subtract), r=[O_, msb], w=[osq])
                                yield
                                op("act", lambda e: e.activation(tq[:], tq[:], AF.Exp, scale=-0.5), r=[tq], w=[tq])
                                op("dve", lambda e: e.tensor_tensor(osq[:], osq[:], tq[:], ALU.mult), r=[osq, tq], w=[osq])
                                yield
                                op("act", lambda e: e.activation(osq[:], osq[:], AF.Identity, scale=vec[:, V_XG + hp:V_XG + hp + 1], bias=vec[:, V_XB + hp:V_XB + hp + 1]),
                                   r=[osq, vec], w=[osq])
                                op("dve", lambda e: e.tensor_tensor(osq[:], osq[:], bon[hi][n_][:], ALU.add), r=[osq, bon[hi][n_]], w=[osq])
                                yield
                                op("pool", lambda e: e.tensor_tensor(mixT[:, 8 + hp, cs], osq[:], gv[hi][n_][:], ALU.mult), r=[osq, gv[hi][n_]], w=[mixT])
                                yield
                            rr([post(hi, hp, sg) for hi, hp in enumerate(hps) for sg in rsegs if sg.out])
            dump("mix%d" % gi, mixT, mixT[:], [128, 16, TG])
            if gi == NG - 1:
                dma("sp", pwkv_o.rearrange("h p i -> p h i"), S32_p[:], r=[S32_p], final=True)
                dma("sp", swkv_o.rearrange("h p i -> p h i"), S32_s[:], r=[S32_s], final=True)
                dma("sp", pshift_o, carry_p[:], r=[carry_p], final=True)
                dma("sp", sshift_o, carry_s[:], r=[carry_s], final=True)
            if stop == "rw":
                continue
            if gi == 0:
                continue
            with scope():
                h1T = hT
                post_tiles = []
                n = 0
                for sg in segs:
                    for ti, rows in enumerate(sg.rows):
                        if gi >= 2 or (sg.name == "p" and ti == 3):
                            post_tiles.append((sg, ti, rows, n))
                        n += 1
                ntl = n
                with scope():
                    Z = sb("Z", [128, ntl, D])
                    wos = Prefetch(Slots("wos", [128, 16, 512], 2), [w_outd[b_] for b_ in range(4)])
                    for blk in range(4):
                        wt = wos.next()
                        for (sg, ti, rows, idx) in post_tiles:
                            c0 = sg.off + ti * 128
                            pb = bank()

                            def mm(e, pb=pb, wt=wt, c0=c0, rows=rows, blk=blk):
                                r_ = None
                                for kc in range(16):
                                    e.matmul(pb[:rows, 0:512], mixT[:, kc, c0:c0 + rows], wt[:, kc, :], start=(kc == 0), stop=False)
                                for j in range(4):
                                    r_ = e.matmul(pb[:rows, j * 128:(j + 1) * 128], hT[:, blk * 4 + j, c0:c0 + rows], aidentb, start=False, stop=(j == 3))
                                return r_
                            op("pe", mm, r=[mixT, hT, wt, cb], w=[pb])
                            if (idx + blk) % 2 == 0:
                                op("act", lambda e, pb=pb, idx=idx, rows=rows, blk=blk: e.copy(Z[:rows, idx, blk * 512:(blk + 1) * 512], pb[:rows, 0:512]), r=[pb], w=[Z])
                            else:
                                op("dve", lambda e, pb=pb, idx=idx, rows=rows, blk=blk: e.tensor_copy(Z[:rows, idx, blk * 512:(blk + 1) * 512], pb[:rows, 0:512]), r=[pb], w=[Z])
                    xnbs = [sb("xnb1_0", [128, D], BF16), sb("xnb1_1", [128, D], BF16)]
                    pend = [None]
                    for k_, (sg, ti, rows, idx) in enumerate(post_tiles):
                        zt = T(Z.h[:, idx, :], Z.b)
                        mv = ln_stats(zt, rows, "l1")
                        xnb = xnbs[k_ % 2]
                        op("act", lambda e, zt=zt, xnb=xnb, mv=mv, rows=rows: e.activation(
                            xnb[:rows, :], zt[:rows, :], AF.Identity, bias=mv[:rows, 3:4], scale=mv[:rows, 2:3]), r=[Z, mv], w=[xnb])
                        fm = (lambda xnb=xnb, rows=rows, c0=sg.off + ti * 128: to_feature_major(xnb, rows, h1T, c0, V_L1G, V_L1B))
                        if pend[0] is not None:
                            pend[0]()
                        pend[0] = fm
                    if pend[0] is not None:
                        pend[0]()
                dump("h1T%d" % gi, h1T, h1T[:], [128, 16, TG])
                if gi == 1:
                    op("dve", lambda e: e.tensor_copy(h1halo[:], h1T[:, :, GT - 2:GT]), r=[h1T], w=[h1halo])
                    continue
                if stop == "ln1":
                    continue
                with scope():
                    R = sb("R", [128, ntl, D])
                    g2bc = sb("g2bc", [128, D])
                    b2bc = sb("b2bc", [128, D])
                    dma("sp", g2bc[:], ln2g_d.partition_broadcast(128), w=[g2bc])
                    dma("sp", b2bc[:], ln2b_d.partition_broadcast(128), w=[b2bc])
                    actT = [sb("actT0", [128, 4, TG], BF16), sb("actT1", [128, 4, TG], BF16)]
                    wds = Prefetch(Slots("wds", [128, 4, 2048], 2), [w_dn[p_] for p_ in range(11)])
                    upq = Prefetch(wsl, [w_up[2 * (4 * p_ + j_) + g_] for p_ in range(11) for j_ in range(4) for g_ in range(2)])
                    Us = [sb("Ub%d" % i, [128, TG + 4]) for i in range(4)]
                    accs = [sb("acc%d" % i, [128, TG]) for i in range(4)]

                    def segviews(t_):
                        d_ = {}
                        for sg_ in segs:
                            b_ = Buf(t_.b.name + "_" + sg_.name, grave)
                            scope_bufs[-1].append(b_)
                            d_[sg_.name] = T(t_.h, b_)
                        return d_
                    Useg = [segviews(t_) for t_ in Us]
                    aseg = [segviews(t_) for t_ in accs]
                    if gi == NG - 1:
                        build.sbuf_left_ffn = nc.sbuf_bytes_remaining
                    for si_, sg in enumerate(segs):
                        sg.uoff = sg.off + 2 * si_
                        sg.cvc = cvc_p if sg.name == "p" else cvc_s
                    def emit_up(pi):
                        aT = actT[pi % 2]
                        for j in range(4):
                            for gvx in range(2):
                                ft = 2 * (4 * pi + j) + gvx
                                wt = upq.next()
                                if gi == 2:
                                    ph = bank()

                                    def mmh(e, ph=ph, wt=wt):
                                        r_ = None
                                        for kc in range(16):
                                            r_ = e.matmul(ph[:, 0:2], wt[:, kc, :], h1halo[:, kc, :], start=(kc == 0), stop=(kc == 15))
                                        return r_
                                    op("pe", mmh, r=[wt, h1halo], w=[ph])
                                    op("dve", lambda e, ph=ph, ft=ft: e.tensor_scalar(cvc_p[:, ft, :], ph[:, 0:2], flag[:, 0:1], None, ALU.mult), r=[ph, flag], w=[cvc_p])
                                for sg in segs:
                                    T_ = sg.T
                                    pb = bank()

                                    def mm(e, pb=pb, wt=wt, sg=sg, T_=T_):
                                        r_ = None
                                        for kc in range(16):
                                            r_ = e.matmul(pb[:, 0:T_], wt[:, kc, :], h1T[:, kc, sg.off:sg.off + T_], start=(kc == 0), stop=(kc == 15))
                                        return r_
                                    op("pe", mm, r=[wt, h1T], w=[pb])
                                    uo = sg.uoff
                                    U = Useg[(j % 2) * 2 + gvx][sg.name]
                                    acc = aseg[(j % 2) * 2 + gvx][sg.name]
                                    cs = slice(sg.off, sg.off + T_)
                                    op("act", lambda e, pb=pb, uo=uo, T_=T_, U=U: e.copy(U[:, uo + 2:uo + 2 + T_], pb[:, 0:T_]), r=[pb], w=[U])
                                    op("dve", lambda e, uo=uo, U=U, sg=sg, ft=ft: e.tensor_copy(U[:, uo:uo + 2], sg.cvc[:, ft, :]), r=[sg.cvc], w=[U])
                                    op("dve", lambda e, uo=uo, U=U, sg=sg, ft=ft, T_=T_: e.tensor_copy(sg.cvc[:, ft, :], U[:, uo + T_:uo + T_ + 2]), r=[U], w=[sg.cvc])
                                    op("act", lambda e, uo=uo, U=U, T_=T_, acc=acc, cs=cs, ft=ft: e.activation(
                                        acc[:, cs], U[:, uo + 2:uo + 2 + T_], AF.Identity, bias=cvb[:, ft:ft + 1], scale=cvw[:, ft, 2:3]), r=[U, cvb, cvw], w=[acc])
                                    op("dve", lambda e, uo=uo, U=U, T_=T_, acc=acc, cs=cs, ft=ft: e.scalar_tensor_tensor(
                                        acc[:, cs], U[:, uo + 1:uo + 1 + T_], cvw[:, ft, 1:2], acc[:, cs], ALU.mult, ALU.add), r=[U, cvw, acc], w=[acc])
                                    op("dve", lambda e, uo=uo, U=U, T_=T_, acc=acc, cs=cs, ft=ft: e.scalar_tensor_tensor(
                                        acc[:, cs], U[:, uo:uo + T_], cvw[:, ft, 0:1], acc[:, cs], ALU.mult, ALU.add), r=[U, cvw, acc], w=[acc])
                                    if gvx == 0:
                                        op("act", lambda e, acc=acc, cs=cs: e.activation(acc[:, cs], acc[:, cs], AF.Gelu), r=[acc], w=[acc])
                                    else:
                                        a0_, a1_ = aseg[(j % 2) * 2][sg.name], aseg[(j % 2) * 2 + 1][sg.name]
                                        op("dve", lambda e, cs=cs, aT=aT, j=j, a0_=a0_, a1_=a1_: e.tensor_tensor(aT[:, j, cs], a0_[:, cs], a1_[:, cs], ALU.mult),
                                           r=[a0_, a1_], w=[aT])

                    def emit_down(pi):
                        aT = actT[pi % 2]
                        wd = wds.next()
                        for (sg, ti, rows, idx) in post_tiles:
                            c0 = sg.off + ti * 128
                            for cb4 in range(4):
                                pb = bank()

                                def mmd(e, pb=pb, wd=wd, c0=c0, rows=rows, cb4=cb4, aT=aT, pi=pi):
                                    r_ = None
                                    for j in range(4):
                                        r_ = e.matmul(pb[:rows, 0:512], aT[:, j, c0:c0 + rows], wd[:, j, cb4 * 512:(cb4 + 1) * 512], start=(j == 0),
                                                      stop=(j == 3 and pi != 0))
                                    if pi == 0:
                                        for j in range(4):
                                            r_ = e.matmul(pb[:rows, j * 128:(j + 1) * 128], h1T[:, cb4 * 4 + j, c0:c0 + rows], aidentb, start=False, stop=(j == 3))
                                    return r_
                                op("pe", mmd, r=[aT, wd, h1T, cb], w=[pb])
                                if pi == 0:
                                    op("act", lambda e, pb=pb, idx=idx, rows=rows, cb4=cb4: e.copy(R[:rows, idx, cb4 * 512:(cb4 + 1) * 512], pb[:rows, 0:512]), r=[pb], w=[R])
                                else:
                                    op("dve", lambda e, pb=pb, idx=idx, rows=rows, cb4=cb4: e.tensor_tensor(
                                        R[:rows, idx, cb4 * 512:(cb4 + 1) * 512], pb[:rows, 0:512], R[:rows, idx, cb4 * 512:(cb4 + 1) * 512], ALU.add), r=[pb, R], w=[R])

                    emit_up(0)
                    for pi in range(11):
                        if pi + 1 < 11:
                            emit_up(pi + 1)
                        emit_down(pi)
                    rtiles = {}
                    for (sg, ti, rows, idx) in post_tiles:
                        b_ = Buf("Rt%d" % idx)
                        b_.w = R.b.w
                        b_.r = dict(R.b.r)
                        scope_bufs[-1].append(b_)
                        rtiles[idx] = T(R.h[:, idx, :], b_)
                    pend = [None]
                    for (sg, ti, rows, idx) in post_tiles:
                        rt = rtiles[idx]
                        mv = ln_stats(rt, rows, "l2")
                        op("act", lambda e, rt=rt, mv=mv, rows=rows: e.activation(
                            rt[:rows, :], rt[:rows, :], AF.Identity, bias=mv[:rows, 3:4], scale=mv[:rows, 2:3]), r=[rt, mv], w=[rt])

                        def tail(rt=rt, rows=rows, sg=sg, ti=ti):
                            op("dve", lambda e: e.tensor_tensor(rt[:rows, :], rt[:rows, :], g2bc[:rows, :], ALU.mult), r=[rt, g2bc], w=[rt])
                            op("dve", lambda e: e.tensor_tensor(rt[:rows, :], rt[:rows, :], b2bc[:rows, :], ALU.add), r=[rt, b2bc], w=[rt])
                            if sg.name == "p":
                                r0 = (gi - 2) * GT + ti * 128
                                dma("sp", y_main[r0:r0 + rows, :], rt[:rows, :], r=[rt], final=True)
                            else:
                                dma("sp", y_s[0:rows, :], rt[:rows, :], r=[rt], final=True)
                        if pend[0] is not None:
                            pend[0]()
                        pend[0] = tail
                    if pend[0] is not None:
                        pend[0]()
                    if gi == NG - 1:
                        dma("sp", pconv_o, cvc_p[:], r=[cvc_p], final=True)
                        dma("sp", sconv_o, cvc_s[:], r=[cvc_s], final=True)
    S.finish()
    root.close()
    build.last_ninst = dict(S.ninst)
    build.nsem = len(S.sems)
    return nc, dbg_out


def _TtView(t):
    return t


RW_R, RW_WD, RW_K, RW_V, RW_AD, RW_GD = 0, 1024, 1088, 2112, 3136, 3200


def rw_tile_cols():
    tiles = []
    tiles.append(np.concatenate([np.arange(RW_WD, RW_WD + 64), np.arange(RW_AD, RW_AD + 64)]))
    tiles.append(np.arange(RW_GD, RW_GD + 128))
    tiles.append(np.concatenate([np.arange(RW_GD + 128, RW_GD + 160), -np.ones(96, np.int64)]))
    for hp in range(8):
        for base in (RW_R, RW_K, RW_V):
            tiles.append(np.arange(base + hp * 128, base + (hp + 1) * 128))
    return tiles


def att_block_cols():
    blocks = []
    for i in range(4):
        heads = [2 * i, 8 + 2 * i, 2 * i + 1, 9 + 2 * i]
        blocks.append(np.concatenate([np.arange(h * 64, h * 64 + 64) for h in heads]))
    blocks.append(np.arange(1024, 1280))
    return blocks


def mix_row_order():
    rows = []
    for g in range(2):
        for i in range(4):
            for h in (8 * g + i, 8 * g + 4 + i):
                rows.append(np.arange(h * 64, h * 64 + 64))
    rows.append(np.arange(1024, 2048))
    return np.concatenate(rows)


def ktile(w, cols):
    sel = np.where(cols >= 0, cols, 0)
    t = w[:, sel].copy()
    t[:, cols < 0] = 0
    return np.ascontiguousarray(t.reshape(16, 128, len(cols)).transpose(1, 0, 2))


def make_consts():
    c = np.zeros((128, 1024), np.float32)
    c[:, 0:128] = np.eye(128)
    for e in range(2):
        c[e * 64:(e + 1) * 64, 128 + e * 64:128 + (e + 1) * 64] = 1
    p = np.arange(128) % 64
    s = np.arange(64)
    c[:, 256:320] = (p[:, None] > s[None, :])
    c[:, 320:384] = (p[:, None] < s[None, :])
    c[:, 384:448] = (p[:, None] <= s[None, :])
    t = np.arange(512)
    c[:, 448:960] = (t % 64 != 0)[None, :]
    return c


def rope_table(pos):
    half = 32
    inv = 10000.0 ** (-np.arange(half, dtype=np.float64) / half)
    ang = pos.astype(np.float64)[:, None] * inv[None, :]
    cos = np.cos(ang).astype(np.float32)
    sin = np.sin(ang).astype(np.float32)
    return np.concatenate([cos, cos, sin, sin], 1).astype(np.float32)


def prep_shared(inp):
    sh = {}
    w_in = inp["w_in"][0]
    sh["w_att"] = np.stack([ktile(w_in, cols) for cols in att_block_cols()])
    w_rw = w_in[:, 1280:]
    rwt = rw_tile_cols()
    sh["w_rw"] = np.stack([ktile(w_rw, cols) for cols in rwt])
    wo = inp["w_out"][0][mix_row_order(), :]
    sh["w_out"] = np.stack([np.ascontiguousarray(wo[:, i * 512:(i + 1) * 512].reshape(16, 128, 512).transpose(1, 0, 2)) for i in range(4)])
    wu = inp["ffn_w_up"][0]
    upcols = []
    for j in range(44):
        upcols.append(np.arange(j * 128, (j + 1) * 128))
        upcols.append(np.arange(DFF + j * 128, DFF + (j + 1) * 128))
    sh["upcols"] = upcols
    sh["w_up"] = np.stack([ktile(wu, cols) for cols in upcols])
    wd = inp["ffn_w_down"][0]
    sh["w_dn"] = np.ascontiguousarray(wd.reshape(11, 4, 128, D).transpose(0, 2, 1, 3))
    vec = np.zeros((128, 160), np.float32)

    def put(col, v, n):
        vec[:, col:col + n] = np.asarray(v, np.float32).reshape(n, 128).T
    put(0, inp["ln_in_g"], 16)
    put(16, inp["ln_in_b"], 16)
    put(32, inp["ln1_g"][0], 16)
    put(48, inp["ln1_b"][0], 16)
    mu = inp["rw_mu"][0]
    for t, cols in enumerate(rwt):
        sel = np.where(cols >= 0, cols, 0)
        v = mu[sel].copy()
        v[cols < 0] = 0
        vec[:, 64 + t] = v
    put(91, inp["rw_w0"][0], 8)
    put(99, inp["rw_a0"][0], 8)
    put(107, inp["rw_k_k"][0], 8)
    put(115, inp["rw_k_a"][0], 8)
    put(123, inp["rw_r_k"][0].reshape(-1), 8)
    put(131, inp["rw_lnx_g"][0], 8)
    put(139, inp["rw_lnx_b"][0], 8)
    sh["vecs"] = vec
    sh["lora_wa"] = np.concatenate([inp["rw_w2"][0], inp["rw_a2"][0]], 0).astype(np.float32)
    sh["lora_g"] = np.ascontiguousarray(inp["rw_g2"][0][0:128])
    sh["lora_gb"] = np.ascontiguousarray(inp["rw_g2"][0][128:160])
    sh["sinks"] = np.ascontiguousarray(inp["attn_sinks"][0])
    sh["ln2g"] = np.ascontiguousarray(inp["ln2_g"][0])
    sh["ln2b"] = np.ascontiguousarray(inp["ln2_b"][0])
    cw = inp["ffn_conv_w"][0]
    cbias = inp["ffn_conv_b"][0]
    sh["convw"] = np.ascontiguousarray(np.stack([cw[:, cols].T for cols in upcols], 1))
    sh["convb"] = np.ascontiguousarray(np.stack([cbias[cols] for cols in upcols], 1))
    sh["consts"] = make_consts()
    sh["rwt"] = rwt
    return sh


def prep_core(inp, sh, c):
    b, g = c // 2, c % 2
    xp = inp["x_prompt"][b]
    m = {}
    if g == 1:
        m["xseq"] = np.ascontiguousarray(xp)
        pos = np.arange(NT)
    else:
        m["xseq"] = np.ascontiguousarray(np.concatenate([xp[1024:], xp[:1024]], 0))
        pos = np.concatenate([np.arange(1024), np.arange(1024)])
    m["xsmp"] = np.ascontiguousarray(inp["x_sample"][c])
    m["flag"] = np.full((128, 1), float(g), np.float32)
    m["cs_p"] = rope_table(pos)
    m["cs_s"] = rope_table(1024 + np.arange(TS))
    m["cache_k"] = np.ascontiguousarray(inp["cache_k"][0, c].reshape(128, 128))
    m["cache_v"] = np.ascontiguousarray(inp["cache_v"][0, c].reshape(128, 128))
    st = inp["state_wkv"][0, c]
    m["swkv_in"] = np.ascontiguousarray(st.transpose(0, 2, 1).reshape(8, 128, 64))
    sf = inp["state_shift"][0, c, 0]
    t = np.zeros((128, NRW), np.float32)
    for ti, cols in enumerate(sh["rwt"]):
        sel = np.where(cols >= 0, cols, 0)
        v = sf[sel].copy()
        v[cols < 0] = 0
        t[:, ti] = v
    m["sshift_in"] = t
    cvs = inp["state_ffn_conv"][0, c]
    m["sconv_in"] = np.ascontiguousarray(np.stack([cvs[:, cols].T for cols in sh["upcols"]], 1))
    for k in ("w_att", "w_rw", "w_out", "w_up", "w_dn", "vecs", "lora_wa", "lora_g", "lora_gb", "sinks", "ln2g", "ln2b",
              "convw", "convb", "consts"):
        m[k] = sh[k]
    return m


_CACHE = {}


def kernel(**inputs):
    inp = {k: np.asarray(v) for k, v in inputs.items()}
    if "nc" not in _CACHE:
        _CACHE["nc"] = build()[0]
    nc = _CACHE["nc"]
    sh = prep_shared(inp)
    in_maps = [prep_core(inp, sh, c) for c in range(8)]
    res = run_bass_kernel_spmd(nc, in_maps, core_ids=list(range(8)))
    R = res.results
    f32 = np.float32
    y_prompt = np.zeros((4, 2048, D), f32)
    y_sample = np.zeros((8, TS, D), f32)
    p_k = np.zeros((1, 4, 128, 2, 64), f32)
    p_v = np.zeros((1, 4, 128, 2, 64), f32)
    p_wkv = np.zeros((1, 4, 16, 64, 64), f32)
    p_shift = np.zeros((1, 4, 1, 3360), f32)
    p_conv = np.zeros((1, 4, 2, 2 * DFF), f32)
    s_k = np.zeros((1, 8, 128, 2, 64), f32)
    s_v = np.zeros((1, 8, 128, 2, 64), f32)
    s_wkv = np.zeros((1, 8, 16, 64, 64), f32)
    s_shift = np.zeros((1, 8, 1, 3360), f32)
    s_conv = np.zeros((1, 8, 2, 2 * DFF), f32)
    rwt = sh["rwt"]
    upc = sh["upcols"]

    def unwkv(a):
        a = np.asarray(a, f32).reshape(8, 2, 64, 64)
        return a.transpose(0, 1, 3, 2).reshape(16, 64, 64)

    def unshift(a):
        a = np.asarray(a, f32)
        o = np.zeros(3360, f32)
        for t, cols in enumerate(rwt):
            ok = cols >= 0
            o[cols[ok]] = a[ok, t]
        return o

    def unconv(a):
        a = np.asarray(a, f32)
        o = np.zeros((2, 2 * DFF), f32)
        for ft, cols in enumerate(upc):
            o[:, cols] = a[:, ft, :].T
        return o
    for c in range(8):
        b, g = c // 2, c % 2
        r = R[c]
        y_prompt[b, g * 1024:(g + 1) * 1024] = np.asarray(r["y_main"], f32)
        y_sample[c] = np.asarray(r["y_s"], f32)
        if g == 1:
            p_k[0, b] = np.asarray(r["pk"], f32).reshape(128, 2, 64)
            p_v[0, b] = np.asarray(r["pv"], f32).reshape(128, 2, 64)
            p_wkv[0, b] = unwkv(r["pwkv"])
            p_shift[0, b, 0] = unshift(r["pshift"])
            p_conv[0, b] = unconv(r["pconv"])
        s_k[0, c] = np.asarray(r["sk"], f32).reshape(128, 2, 64)
        s_v[0, c] = np.asarray(r["sv"], f32).reshape(128, 2, 64)
        s_wkv[0, c] = unwkv(r["swkv"])
        s_shift[0, c, 0] = unshift(r["sshift"])
        s_conv[0, c] = unconv(r["sconv"])
    return (y_prompt, y_sample, p_k, p_v, p_wkv, p_shift, p_conv, s_k, s_v, s_wkv, s_shift, s_conv)
```

```python
from contextlib import ExitStack
import numpy as np
import concourse.bass as bass
import concourse.mybir as mybir
from concourse.bass_utils import run_bass_kernel_spmd

F32 = mybir.dt.float32
BF16 = mybir.dt.bfloat16
AF = mybir.ActivationFunctionType
ALU = mybir.AluOpType
AX = mybir.AxisListType

ENGS = ("pe", "act", "dve", "pool", "sp")
EPOCH = 12000

D = 2048
NT = 2048
GT = 512
NG = 4
TS = 32
DFF = 5632
NFT = 88
ALPHA = 2 ** 0.25


def _bf16_round(x):
    u = int(np.array([x], np.float32).view(np.uint32)[0])
    r = ((u + 0x7FFF + ((u >> 16) & 1)) >> 16) << 16
    return float(np.array([r], np.uint32).view(np.float32)[0])


ALPHA_HI = _bf16_round(ALPHA)
ALPHA_LO = ALPHA - ALPHA_HI
ATT_SCALE = 0.125
LN_EPS = 1e-5
LNX_EPS = 64e-5
C0 = float(np.exp(-0.5))
NRW = 27


class Buf:
    __slots__ = ("name", "w", "r", "dsem")

    def __init__(self, name, grave=None):
        self.name = name
        self.w = None
        self.r = dict(grave) if grave else {}
        self.dsem = None


class Sched:
    def __init__(self, nc):
        self.nc = nc
        self.streams = {e: [] for e in ENGS}
        self.count = {}
        self.waited = {e: {} for e in ENGS}
        self.sems = {}
        self.ctx = []
        self.epoch = {e: 0 for e in ENGS}
        for e in ENGS:
            self._mksem(("E", e, 0), "sem_%s0" % e)
        self.ndsem = 0
        self.final = []
        self.ninst = {e: 0 for e in ENGS}

    def _mksem(self, key, name):
        cm = self.nc.semaphore(name)
        h = cm.__enter__()
        self.ctx.append(cm)
        self.sems[key] = h
        self.count[key] = 0
        return h

    def ekey(self, eng):
        key = ("E", eng, self.epoch[eng])
        if self.count[key] >= EPOCH:
            self.epoch[eng] += 1
            key = ("E", eng, self.epoch[eng])
            self._mksem(key, "sem_%s%d" % (eng, self.epoch[eng]))
        return key

    def dkey(self, buf):
        if buf.dsem is None or self.count[buf.dsem] >= 2 * EPOCH:
            key = ("D", self.ndsem)
            self.ndsem += 1
            self._mksem(key, "dsem%d" % key[1])
            buf.dsem = key
        return buf.dsem

    def _needs(self, eng, reads, writes):
        need = {}
        wd = self.waited[eng]

        def add(kc):
            k, c = kc
            if k[0] == "E" and k[1] == "pe" and eng == "pe":
                return
            if wd.get(k, 0) >= c:
                return
            if need.get(k, 0) < c:
                need[k] = c
        for b in reads:
            if b.w is not None:
                add(b.w)
        for b in writes:
            if b.w is not None:
                add(b.w)
            for k, c in b.r.items():
                add((k, c))
        st = self.streams[eng]
        for k, c in need.items():
            wd[k] = c
            sem = self.sems[k]
            st.append(lambda e, sem=sem, c=c: e.wait_ge(sem, c))

    def _mark(self, key, c, reads, writes):
        for b in reads:
            if b.r.get(key, 0) < c:
                b.r[key] = c
        for b in writes:
            b.w = (key, c)
            b.r = {}

    def op(self, eng, fn, reads=(), writes=()):
        self._needs(eng, reads, writes)
        key = self.ekey(eng)
        self.count[key] += 1
        c = self.count[key]
        sem = self.sems[key]
        rec = _Rec()
        fn(rec)
        calls = rec.calls
        self.streams[eng].append(lambda e, calls=calls, sem=sem: _replay(e, calls).then_inc(sem, 1))
        self.ninst[eng] += 1
        self._mark(key, c, reads, writes)

    def dma(self, eng, out_ap, in_ap, reads=(), writes=(), track=None, final=False):
        self._needs(eng, reads, writes)
        tb = track if track is not None else (writes[0] if writes else reads[0])
        key = self.dkey(tb)
        self.count[key] += 16
        c = self.count[key]
        sem = self.sems[key]
        self.streams[eng].append(lambda e, o=out_ap, i=in_ap, sem=sem: e.dma_start(out=o, in_=i).then_inc(sem, 16))
        self.ninst[eng] += 1
        self._mark(key, c, reads, writes)
        if final:
            self.final.append((key, c))

    def finish(self):
        fin = {}
        for k, c in self.final:
            fin[k] = max(fin.get(k, 0), c)
        for k, c in fin.items():
            sem = self.sems[k]
            self.streams["sp"].append(lambda e, sem=sem, c=c: e.wait_ge(sem, c))
        streams = self.streams
        with self.nc.Block() as block:
            @block.tensor
            def _(e):
                for f in streams["pe"]:
                    f(e)

            @block.scalar
            def _(e):
                for f in streams["act"]:
                    f(e)

            @block.vector
            def _(e):
                for f in streams["dve"]:
                    f(e)

            @block.gpsimd
            def _(e):
                for f in streams["pool"]:
                    f(e)

            @block.sync
            def _(e):
                for f in streams["sp"]:
                    f(e)
        for cm in reversed(self.ctx):
            cm.__exit__(None, None, None)


class _Rec:
    def __init__(self):
        self.calls = []

    def __getattr__(self, name):
        def f(*a, **k):
            self.calls.append((name, a, k))
            return self
        return f


def _replay(e, calls):
    r = None
    for name, a, k in calls:
        r = getattr(e, name)(*a, **k)
    return r


class T:
    def __init__(self, h, b):
        self.h = h
        self.b = b

    def __getitem__(self, k):
        return self.h[k]


class Seg:
    def __init__(self, name, T_, C, off):
        self.name = name
        self.T = T_
        self.C = C
        self.nch = T_ // C
        self.off = off
        self.rows = [min(128, T_ - i * 128) for i in range((T_ + 127) // 128)]
        self.ntile = len(self.rows)


def build(debug=None, ngroups=NG, stop=None):
    nc = bass.Bass("TRN2", target_bir_lowering=False)
    S = Sched(nc)
    root = ExitStack()
    stack = [root]
    grave = {}
    scope_bufs = [[]]
    dbg_out = {}
    uid = [0]

    def din(name, shape):
        return nc.dram_tensor(name, list(shape), F32, kind="ExternalInput").ap()

    def dout(name, shape):
        return nc.dram_tensor(name, list(shape), F32, kind="ExternalOutput").ap()

    def sb(name, shape, dt=F32):
        uid[0] += 1
        name = "%s_u%d" % (name, uid[0])
        h = stack[-1].enter_context(nc.sbuf_tensor(name, list(shape), dt))
        b = Buf(name, grave)
        scope_bufs[-1].append(b)
        return T(h, b)

    class scope:
        def __enter__(self):
            st = ExitStack()
            stack.append(st)
            scope_bufs.append([])
            return self

        def __exit__(self, *a):
            for b in scope_bufs.pop():
                for kc in ([b.w] if b.w else []) + list(b.r.items()):
                    if grave.get(kc[0], 0) < kc[1]:
                        grave[kc[0]] = kc[1]
            stack.pop().close()
            return False

    def op(eng, fn, r=(), w=()):
        S.op(eng, fn, [t.b for t in r], [t.b for t in w])

    def dma(eng, o, i, r=(), w=(), final=False):
        S.dma(eng, o, i, [t.b for t in r], [t.b for t in w], final=final)

    def dump(name, t, ap, shape):
        if debug is None or name not in debug:
            return
        o = dout("dbg_" + name, shape)
        dbg_out[name] = shape
        dma("pool", o, ap, r=[t], final=True)

    xseq = din("xseq", [NT, D])
    xsmp = din("xsmp", [TS, D])
    flag_d = din("flag", [128, 1])
    cs_p = din("cs_p", [NT, 128])
    cs_s = din("cs_s", [TS, 128])
    cache_k = din("cache_k", [128, 128])
    cache_v = din("cache_v", [128, 128])
    swkv_in = din("swkv_in", [8, 128, 64])
    sshift_in = din("sshift_in", [128, NRW])
    sconv_in = din("sconv_in", [128, NFT, 2])
    w_att = din("w_att", [5, 128, 16, 256])
    w_rw = din("w_rw", [NRW, 128, 16, 128])
    w_outd = din("w_out", [4, 128, 16, 512])
    w_up = din("w_up", [NFT, 128, 16, 128])
    w_dn = din("w_dn", [11, 128, 4, 2048])
    vecs = din("vecs", [128, 160])
    lora_wa = din("lora_wa", [128, 1024])
    lora_g = din("lora_g", [128, 1024])
    lora_gb = din("lora_gb", [32, 1024])
    sinks_d = din("sinks", [16])
    ln2g_d = din("ln2g", [D])
    ln2b_d = din("ln2b", [D])
    convw_d = din("convw", [128, NFT, 3])
    convb_d = din("convb", [128, NFT])
    consts = din("consts", [128, 1024])

    y_main = dout("y_main", [1024, D])
    y_s = dout("y_s", [TS, D])
    pk_o = dout("pk", [128, 128])
    pv_o = dout("pv", [128, 128])
    sk_o = dout("sk", [128, 128])
    sv_o = dout("sv", [128, 128])
    pwkv_o = dout("pwkv", [8, 128, 64])
    swkv_o = dout("swkv", [8, 128, 64])
    pshift_o = dout("pshift", [128, NRW])
    sshift_o = dout("sshift", [128, NRW])
    pconv_o = dout("pconv", [128, NFT, 2])
    sconv_o = dout("sconv", [128, NFT, 2])

    cst = sb("cst", [128, 1024])
    dma("sp", cst[:], consts, w=[cst])
    identf = cst[:, 0:128]
    bones = cst[:, 128:256]
    mL = cst[:, 256:320]
    mUI = cst[:, 320:448]
    rmask = cst[:, 448:960]
    cb = sb("cb", [128, 512], BF16)
    op("dve", lambda e: e.tensor_copy(cb[:, 0:128], cst[:, 0:128]), r=[cst], w=[cb])
    op("dve", lambda e: e.tensor_scalar(cb[:, 128:256], cst[:, 0:128], ALPHA, None, ALU.mult), r=[cst], w=[cb])
    op("dve", lambda e: e.memset(cb[:, 256:384], 1.0), w=[cb])
    op("dve", lambda e: e.tensor_scalar(cb[:, 384:512], cst[:, 0:128], ALPHA_LO, None, ALU.mult), r=[cst], w=[cb])
    op("dve", lambda e: e.tensor_tensor(cb[:, 320:384], cst[:, 0:64], cst[:, 64:128], ALU.add), r=[cst], w=[cb])
    identb = cb[:, 0:128]
    aidentb = cb[:, 128:256]
    aidentb_lo = cb[:, 384:512]
    onesb = cb[:, 256:320]
    bones64 = sb("bones64", [128, 128])
    op("dve", lambda e: e.tensor_scalar(bones64[:], cst[:, 128:256], 1.0 / 64, None, ALU.mult), r=[cst], w=[bones64])
    vec = sb("vec", [128, 160])
    dma("sp", vec[:], vecs, w=[vec])
    V_LNG, V_LNB, V_L1G, V_L1B = 0, 16, 32, 48
    V_MU, V_W0, V_A0, V_KK, V_KA, V_RK, V_XG, V_XB = 64, 91, 99, 107, 115, 123, 131, 139
    omu = sb("omu", [128, NRW])
    nka = sb("nka", [128, 8])
    flag = sb("flagt", [128, 1])
    dma("sp", flag[:], flag_d, w=[flag])
    lwa = sb("lwa", [128, 1024], BF16)
    lg = sb("lg", [128, 1024], BF16)
    lgb = sb("lgb", [32, 1024], BF16)
    dma("pool", lwa[:], lora_wa, w=[lwa])
    dma("pool", lg[:], lora_g, w=[lg])
    dma("pool", lgb[:], lora_gb, w=[lgb])
    es = sb("es", [128, 16])
    dma("sp", es[:], sinks_d.partition_broadcast(128), w=[es])
    op("act", lambda e: e.activation(es[:], es[:], AF.Exp), r=[es], w=[es])
    esb = sb("esb", [128, 2, 4, 64])
    for g in range(2):
        for hh in range(2):
            for i in range(4):
                hd = 8 * g + 4 * hh + i
                op("dve", lambda e, g=g, hh=hh, i=i, hd=hd: e.tensor_copy(
                    esb[hh * 64:(hh + 1) * 64, g, i, :], es[hh * 64:(hh + 1) * 64, hd:hd + 1].to_broadcast([64, 64])),
                   r=[es], w=[esb])
    cvw = sb("cvw", [128, NFT, 3])
    cvb = sb("cvb", [128, NFT])
    dma("sp", cvw[:], convw_d, w=[cvw])
    dma("sp", cvb[:], convb_d, w=[cvb])
    h1halo = sb("h1halo", [128, 16, 2], BF16)

    banks = []
    for i in range(8):
        h = root.enter_context(nc.psum_tensor("pb%d" % i, [128, 512], F32))
        banks.append(T(h, Buf("pb%d" % i)))
    bank_i = [0]

    def bank():
        t = banks[bank_i[0] % 8]
        bank_i[0] += 1
        return t

    class Slots:
        def __init__(self, name, shape, n, dt=BF16):
            self.t = [sb("%s%d" % (name, i), shape, dt) for i in range(n)]
            self.i = 0

        def next(self):
            t = self.t[self.i % len(self.t)]
            self.i += 1
            return t

    wsl = Slots("wsl", [128, 16, 128], 3)

    class Prefetch:
        def __init__(self, slots, aps, depth=None):
            self.t = slots.t
            self.aps = list(aps)
            self.n = len(self.t)
            self.depth = self.n - 1 if depth is None else depth
            self.i = 0
            self.issued = 0
            self._fill()

        def _fill(self):
            while self.issued < len(self.aps) and self.issued <= self.i + self.depth:
                t = self.t[self.issued % self.n]
                dma("pool", t[:], self.aps[self.issued], w=[t])
                self.issued += 1

        def next(self):
            self._fill()
            t = self.t[self.i % self.n]
            self.i += 1
            return t

    kT_seq = sb("kT_seq", [128, NT], BF16)
    vt_seq = sb("vt_seq", [128, 16, 128], BF16)
    kT_s = sb("kT_s", [128, 128 + TS], BF16)
    vt_s = sb("vt_s", [128, 2, 128], BF16)
    carry_p = sb("carry_p", [128, NRW])
    carry_s = sb("carry_s", [128, NRW])
    op("dve", lambda e: e.memset(carry_p[:], 0.0), w=[carry_p])
    dma("sp", carry_s[:], sshift_in, w=[carry_s])
    S32_p = sb("S32_p", [128, 8, 64])
    S32_s = sb("S32_s", [128, 8, 64])
    op("dve", lambda e: e.memset(S32_p[:], 0.0), w=[S32_p])
    dma("sp", S32_s[:], swkv_in.rearrange("h p i -> p h i"), w=[S32_s])
    cvc_p = sb("cvc_p", [128, NFT, 2])
    cvc_s = sb("cvc_s", [128, NFT, 2])
    op("dve", lambda e: e.memset(cvc_p[:], 0.0), w=[cvc_p])
    dma("sp", cvc_s[:], sconv_in, w=[cvc_s])
    op("dve", lambda e: e.tensor_scalar(omu[:], vec[:, V_MU:V_MU + NRW], -1.0, 1.0, ALU.mult, ALU.add), r=[vec], w=[omu])
    op("dve", lambda e: e.tensor_scalar(nka[:], vec[:, V_KA:V_KA + 8], -1.0, None, ALU.mult), r=[vec], w=[nka])
    with scope():
        ck = sb("ck", [128, 128])
        cv_ = sb("cv", [128, 128])
        ckb = sb("ckb", [128, 128], BF16)
        dma("sp", ck[:], cache_k, w=[ck])
        dma("sp", cv_[:], cache_v, w=[cv_])
        op("dve", lambda e: e.tensor_copy(ckb[:], ck[:]), r=[ck], w=[ckb])
        op("dve", lambda e: e.tensor_copy(vt_s[:, 0, :], cv_[:]), r=[cv_], w=[vt_s])
        pb = bank()
        op("pe", lambda e: e.matmul(pb[:, 0:128], ckb[:], identb, start=True, stop=True), r=[ckb, cb], w=[pb])
        op("act", lambda e: e.copy(kT_s[:, 0:128], pb[:, 0:128]), r=[pb], w=[kT_s])
        dma("sp", sk_o[0:96, :], ck[32:128, :], r=[ck], final=True)
        dma("sp", sv_o[0:96, :], cv_[32:128, :], r=[cv_], final=True)

    def ln_stats(x, rows, tag):
        uid[0] += 1
        st = sb("st_%s_%d" % (tag, uid[0]), [128, 24])
        mv = sb("mv_%s_%d" % (tag, uid[0]), [128, 4])
        for j in range(4):
            op("dve", lambda e, j=j: e.bn_stats(st[:rows, j * 6:(j + 1) * 6], x[:rows, j * 512:(j + 1) * 512]), r=[x], w=[st])
        op("dve", lambda e: e.bn_aggr(mv[:rows, 0:2], st[:rows, :]), r=[st], w=[mv])
        op("act", lambda e: e.activation(mv[:rows, 2:3], mv[:rows, 1:2], AF.Ln, bias=LN_EPS, scale=1.0), r=[mv], w=[mv])
        op("act", lambda e: e.activation(mv[:rows, 2:3], mv[:rows, 2:3], AF.Exp, scale=-0.5), r=[mv], w=[mv])
        op("dve", lambda e: e.scalar_tensor_tensor(mv[:rows, 3:4], mv[:rows, 0:1], -1.0, mv[:rows, 2:3], ALU.mult, ALU.mult), r=[mv], w=[mv])
        return mv

    def to_feature_major(xnb, rows, dstT, col0, gcol, bcol):
        for q4 in range(4):
            pb = bank()

            def mm(e, q4=q4, pb=pb):
                r_ = None
                for j in range(4):
                    kc = q4 * 4 + j
                    r_ = e.matmul(pb[:, j * 128:j * 128 + rows], xnb[:rows, kc * 128:(kc + 1) * 128], identb[:rows, :rows],
                                  start=True, stop=True)
                return r_
            op("pe", mm, r=[xnb, cb], w=[pb])
            for j in range(4):
                kc = q4 * 4 + j
                eng = "act" if q4 % 2 == 0 else "dve"
                if eng == "act":
                    op("act", lambda e, j=j, kc=kc, pb=pb: e.activation(
                        dstT[:, kc, col0:col0 + rows], pb[:, j * 128:j * 128 + rows], AF.Identity,
                        bias=vec[:, bcol + kc:bcol + kc + 1], scale=vec[:, gcol + kc:gcol + kc + 1]), r=[pb, vec], w=[dstT])
                else:
                    op("dve", lambda e, j=j, kc=kc, pb=pb: e.tensor_scalar(
                        dstT[:, kc, col0:col0 + rows], pb[:, j * 128:j * 128 + rows],
                        vec[:, gcol + kc:gcol + kc + 1], vec[:, bcol + kc:bcol + kc + 1], ALU.mult, ALU.add), r=[pb, vec], w=[dstT])

    for gi in range(ngroups):
        segs = [Seg("p", GT, 64, 0)]
        if gi == NG - 1:
            segs.append(Seg("s", TS, 32, GT))
        TG = sum(s.T for s in segs)
        full_post = gi >= 2
        with scope():
            hT = sb("hT", [128, 16, TG], BF16)
            mixT = sb("mixT", [128, 16, TG], BF16)
            with scope():
                xts = [sb("xt0", [128, D]), sb("xt1", [128, D])]
                xnbs = [sb("xnb0", [128, D], BF16), sb("xnb1", [128, D], BF16)]
                n = 0
                pend = [None]
                for sg in segs:
                    for ti, rows in enumerate(sg.rows):
                        xt = xts[n % 2]
                        xnb = xnbs[n % 2]
                        n += 1
                        src = xseq[gi * GT + ti * 128: gi * GT + ti * 128 + rows, :] if sg.name == "p" else xsmp[0:rows, :]
                        dma("sp", xt[:rows, :], src, w=[xt])
                        mv = ln_stats(xt, rows, "in%d" % (n % 2))
                        op("act", lambda e, xt=xt, xnb=xnb, mv=mv, rows=rows: e.activation(
                            xnb[:rows, :], xt[:rows, :], AF.Identity, bias=mv[:rows, 3:4], scale=mv[:rows, 2:3]), r=[xt, mv], w=[xnb])
                        fm = (lambda xnb=xnb, rows=rows, c0=sg.off + ti * 128: to_feature_major(xnb, rows, hT, c0, V_LNG, V_LNB))
                        if pend[0] is not None:
                            pend[0]()
                        pend[0] = fm
                if pend[0] is not None:
                    pend[0]()
            dump("hT%d" % gi, hT, hT[:], [128, 16, TG])
            if stop == "ln":
                continue
            with scope():
              if gi >= 1:
                    ntl = sum(s.ntile for s in segs)
                    qtok = sb("qtok", [128, ntl, 1024], BF16)
                    ktok = sb("ktok", [128, ntl, 128])
                    vtok = sb("vtok", [128, ntl, 128])
                    ktb = sb("ktb", [128, ntl, 128], BF16)
                    qT = sb("qT", [128, 8, TG], BF16)
                    cst_t = sb("cs_t", [128, ntl, 128])
                    n = 0
                    for sg in segs:
                        for ti, rows in enumerate(sg.rows):
                            src = cs_p[gi * GT + ti * 128: gi * GT + ti * 128 + rows, :] if sg.name == "p" else cs_s[0:rows, :]
                            dma("sp", cst_t[:rows, n, :], src, w=[cst_t])
                            n += 1
                    wab = Prefetch(Slots("wab", [128, 16, 256], 2), [w_att[b_] for b_ in range(5)])
                    tA = sb("ropeA", [128, 256])
                    tB = sb("ropeB", [128, 256])
                    for blk in range(5):
                        wt = wab.next()
                        n = 0
                        for sg in segs:
                            for ti, rows in enumerate(sg.rows):
                                if gi == 1 and ((blk < 4 and ti < 3) or (blk == 4 and ti < 2)):
                                    n += 1
                                    continue
                                pb = bank()
                                c0 = sg.off + ti * 128

                                def mm(e, pb=pb, wt=wt, c0=c0, rows=rows):
                                    r_ = None
                                    for kc in range(16):
                                        r_ = e.matmul(pb[:rows, 0:256], hT[:, kc, c0:c0 + rows], wt[:, kc, :], start=(kc == 0), stop=(kc == 15))
                                    return r_
                                op("pe", mm, r=[hT, wt], w=[pb])
                                if stop == "attproj1":
                                    op("act", lambda e, pb=pb, rows=rows: e.copy(tA[:rows, :], pb[:rows, 0:256]), r=[pb], w=[tA])
                                    n += 1
                                    continue
                                nh = 4 if blk < 4 else 2
                                w_ = nh * 64
                                cc = cst_t[:rows, n, 0:64].unsqueeze(1).to_broadcast([rows, nh, 64])
                                ss = cst_t[:rows, n, 64:128].unsqueeze(1).to_broadcast([rows, nh, 64])
                                x3 = pb[:rows, 0:w_].rearrange("p (h d) -> p h d", d=64)
                                A3 = tA[:rows, 0:w_].rearrange("p (h d) -> p h d", d=64)
                                B3 = tB[:rows, 0:w_].rearrange("p (h d) -> p h d", d=64)
                                op("dve", lambda e, A3=A3, x3=x3, cc=cc: e.tensor_tensor(A3, x3, cc, ALU.mult), r=[pb, cst_t], w=[tA])
                                op("dve", lambda e, B3=B3, x3=x3, ss=ss: e.tensor_tensor(B3, x3, ss, ALU.mult), r=[pb, cst_t], w=[tB])
                                if blk < 4:
                                    dst = qtok[:rows, n, blk * 256:(blk + 1) * 256].rearrange("p (h d) -> p h d", d=64)
                                    dT = qtok
                                else:
                                    dst = ktok[:rows, n, :].rearrange("p (h d) -> p h d", d=64)
                                    dT = ktok
                                op("dve", lambda e, dst=dst, A3=A3, B3=B3: e.tensor_tensor(dst[:, :, 0:32], A3[:, :, 0:32], B3[:, :, 32:64], ALU.subtract),
                                   r=[tA, tB], w=[dT])
                                op("dve", lambda e, dst=dst, A3=A3, B3=B3: e.tensor_tensor(dst[:, :, 32:64], A3[:, :, 32:64], B3[:, :, 0:32], ALU.add),
                                   r=[tA, tB], w=[dT])
                                if stop == "attproj2":
                                    n += 1
                                    continue
                                lvl = int(stop.split(":")[1]) if (stop and ":" in stop) else 99
                                if blk == 4:
                                    op("dve", lambda e, n=n, rows=rows, pb=pb: e.tensor_copy(vtok[:rows, n, :], pb[:rows, 128:256]), r=[pb], w=[vtok])
                                    if lvl < 1:
                                        n += 1
                                        continue
                                    op("act", lambda e, n=n, rows=rows: e.copy(ktb[:rows, n, :], ktok[:rows, n, :]), r=[ktok], w=[ktb])
                                    if lvl < 2:
                                        n += 1
                                        continue
                                    if sg.name == "p":
                                        gt = gi * 4 + ti
                                        op("dve", lambda e, n=n, gt=gt: e.tensor_copy(vt_seq[:, gt, :], vtok[:, n, :]), r=[vtok], w=[vt_seq])
                                    else:
                                        op("dve", lambda e, n=n, rows=rows: e.tensor_copy(vt_s[:rows, 1, :], vtok[:rows, n, :]), r=[vtok], w=[vt_s])
                                    if lvl < 3:
                                        n += 1
                                        continue
                                    pk = bank()
                                    op("pe", lambda e, pk=pk, n=n, rows=rows: e.matmul(pk[:, 0:rows], ktb[:rows, n, :], identb[:rows, :rows], start=True, stop=True),
                                       r=[ktb, cb], w=[pk])
                                    if sg.name == "p":
                                        g0 = gi * GT + ti * 128
                                        op("act", lambda e, pk=pk, g0=g0: e.copy(kT_seq[:, g0:g0 + 128], pk[:, 0:128]), r=[pk], w=[kT_seq])
                                    else:
                                        op("act", lambda e, pk=pk, rows=rows: e.copy(kT_s[:, 128:128 + rows], pk[:, 0:rows]), r=[pk], w=[kT_s])
                                n += 1
                    if gi == NG - 1:
                        dma("sp", pk_o, ktok[:, 3, :], r=[ktok], final=True)
                        dma("sp", pv_o, vtok[:, 3, :], r=[vtok], final=True)
                        dma("sp", sk_o[96:128, :], ktok[0:TS, 4, :], r=[ktok], final=True)
                        dma("sp", sv_o[96:128, :], vtok[0:TS, 4, :], r=[vtok], final=True)
                    if stop and stop.startswith("attproj"):
                        continue
                    n = 0
                    for sg in segs:
                        for ti, rows in enumerate(sg.rows):
                            c0 = sg.off + ti * 128
                            if gi == 1 and ti < 3:
                                n += 1
                                continue
                            for q4 in range(2):
                                pb = bank()

                                def mm(e, pb=pb, n=n, rows=rows, q4=q4):
                                    r_ = None
                                    for j in range(4):
                                        jj = q4 * 4 + j
                                        r_ = e.matmul(pb[:, j * 128:j * 128 + rows], qtok[:rows, n, jj * 128:(jj + 1) * 128], identb[:rows, :rows],
                                                      start=True, stop=True)
                                    return r_
                                op("pe", mm, r=[qtok, cb], w=[pb])
                                eng = "act" if q4 == 0 else "dve"
                                src = pb[:, :].rearrange("p (j t) -> p j t", t=128)[:, :, 0:rows]
                                dst = qT[:, q4 * 4:q4 * 4 + 4, c0:c0 + rows]
                                if eng == "act":
                                    op("act", lambda e, src=src, dst=dst: e.copy(dst, src), r=[pb], w=[qT])
                                else:
                                    op("dve", lambda e, src=src, dst=dst: e.tensor_copy(dst, src), r=[pb], w=[qT])
                            n += 1
                    dump("qT%d" % gi, qT, qT[:], [128, 8, TG])
                    if stop == "attproj":
                        continue
                    pT = [sb("pT0", [128, 2, 512], BF16), sb("pT1", [128, 2, 512], BF16)]
                    dsum = sb("dsum", [128, 256])
                    npt = 0
                    pending = [None]
                    for sg in segs:
                        Cq = sg.C
                        for c in range(sg.nch):
                            if gi == 1 and c < 6:
                                continue
                            qcols = slice(sg.off + c * Cq, sg.off + (c + 1) * Cq)
                            kb = []
                            if sg.name == "p":
                                cg = gi * 8 + c
                                m = cg // 2
                                if cg % 2 == 0:
                                    if m >= 1:
                                        kb.append((kT_seq, (m - 1) * 128, vt_seq, m - 1, 0, 128, cg in (16, 17)))
                                    kb.append((kT_seq, m * 128, vt_seq, m, 0, 64, False))
                                else:
                                    if m >= 1:
                                        kb.append((kT_seq, (m - 1) * 128 + 64, vt_seq, m - 1, 64, 64, cg in (16, 17)))
                                    kb.append((kT_seq, m * 128, vt_seq, m, 0, 128, False))
                            else:
                                kb.append((kT_s, 0, vt_s, 0, 0, 128, False))
                                kb.append((kT_s, 128, vt_s, 1, 0, TS, False))
                            for g in range(2):
                                N = 8 * Cq
                                pt = pT[npt % 2]
                                npt += 1
                                for bi, (kt, kc0, vt, vti, pr0, nk, uf) in enumerate(kb):
                                    ps_ = bank()
                                    rhs = qT[g * 64:(g + 1) * 64, :, qcols]
                                    op("pe", lambda e, ps_=ps_, kt=kt, kc0=kc0, nk=nk, rhs=rhs, pr0=pr0, g=g, N=N, Cq=Cq: e.matmul(
                                        ps_[pr0:pr0 + nk, 0:N].rearrange("p (h q) -> p h q", q=Cq), kt[g * 64:(g + 1) * 64, kc0:kc0 + nk], rhs,
                                        start=True, stop=True, tile_position=(g * 64, pr0)), r=[kt, qT], w=[ps_])
                                    op("act", lambda e, pt=pt, bi=bi, ps_=ps_, pr0=pr0, nk=nk, N=N: e.activation(
                                        pt[pr0:pr0 + nk, bi, 0:N], ps_[pr0:pr0 + nk, 0:N], AF.Exp, scale=ATT_SCALE), r=[ps_], w=[pt])
                                    if uf:
                                        op("dve", lambda e, pt=pt, bi=bi, pr0=pr0, nk=nk, N=N: e.tensor_scalar(
                                            pt[pr0:pr0 + nk, bi, 0:N], pt[pr0:pr0 + nk, bi, 0:N], flag[pr0:pr0 + nk, 0:1], None, ALU.mult),
                                           r=[pt, flag], w=[pt])
                                def stage2(pt=pt, kb=kb, g=g, Cq=Cq, qcols=qcols):
                                    po = bank()
                                    pd = bank()
                                    Nh = 4 * Cq

                                    def mmo(e, po=po, pd=pd, pt=pt, kb=kb, g=g, Nh=Nh):
                                        r_ = None
                                        for hh in range(2):
                                            for bi, (kt, kc0, vt, vti, pr0, nk, uf) in enumerate(kb):
                                                e.matmul(po[hh * 64:(hh + 1) * 64, 0:Nh], vt[pr0:pr0 + nk, vti, g * 64:(g + 1) * 64],
                                                         pt[pr0:pr0 + nk, bi, hh * Nh:(hh + 1) * Nh], start=(bi == 0), stop=(bi == len(kb) - 1),
                                                         tile_position=(pr0, hh * 64))
                                        for hh in range(2):
                                            for bi, (kt, kc0, vt, vti, pr0, nk, uf) in enumerate(kb):
                                                r_ = e.matmul(pd[hh * 64:(hh + 1) * 64, 0:Nh], onesb[pr0:pr0 + nk, 0:64],
                                                              pt[pr0:pr0 + nk, bi, hh * Nh:(hh + 1) * Nh], start=(bi == 0), stop=(bi == len(kb) - 1),
                                                              tile_position=(pr0, hh * 64))
                                        return r_
                                    vts = list({id(k[2]): k[2] for k in kb}.values())
                                    op("pe", mmo, r=[pt, cb] + vts, w=[po, pd])
                                    op("dve", lambda e, pd=pd, g=g, Nh=Nh, Cq=Cq: e.tensor_tensor(
                                        dsum[:, 0:Nh].rearrange("p (i q) -> p i q", q=Cq), pd[:, 0:Nh].rearrange("p (i q) -> p i q", q=Cq),
                                        esb[:, g, :, 0:Cq], ALU.add), r=[pd, esb], w=[dsum])
                                    op("dve", lambda e, Nh=Nh: e.reciprocal(dsum[:, 0:Nh], dsum[:, 0:Nh]), r=[dsum], w=[dsum])
                                    op("dve", lambda e, po=po, g=g, Nh=Nh, Cq=Cq, qcols=qcols: e.tensor_tensor(
                                        mixT[:, g * 4:g * 4 + 4, qcols], po[:, 0:Nh].rearrange("p (i q) -> p i q", q=Cq),
                                        dsum[:, 0:Nh].rearrange("p (i q) -> p i q", q=Cq), ALU.mult), r=[po, dsum], w=[mixT])
                                if pending[0] is not None:
                                    pending[0]()
                                pending[0] = stage2
                    if pending[0] is not None:
                        pending[0]()
            dump("att%d" % gi, mixT, mixT[:, 0:8, :], [128, 8, TG])
            if stop == "att":
                continue
            if gi == 2:
                op("dve", lambda e: e.tensor_scalar(carry_p[:], carry_p[:], flag[:, 0:1], None, ALU.mult), r=[carry_p, flag], w=[carry_p])
                op("dve", lambda e: e.tensor_scalar(S32_p[:], S32_p[:], flag[:, 0:1], None, ALU.mult), r=[S32_p, flag], w=[S32_p])
            rsegs = []
            for sg in segs:
                if sg.name == "p":
                    for hf in range(2):
                        r_ = Seg("p%d" % hf, GT // 2, 64, hf * (GT // 2))
                        r_.carry, r_.S32 = carry_p, S32_p
                        rsegs.append(r_)
                else:
                    r_ = Seg("s", TS, 32, sg.off)
                    r_.carry, r_.S32 = carry_s, S32_s
                    rsegs.append(r_)

            for r_ in rsegs:
                r_.out = (gi >= 2) or (gi == 1 and r_.name == "p1") or r_.name == "s"
            need_r = any(r_.out for r_ in rsegs)

            def rr(gens):
                gens = list(gens)
                while gens:
                    for g_ in list(gens):
                        try:
                            next(g_)
                        except StopIteration:
                            gens.remove(g_)

            def v3(ap, C):
                return ap.rearrange("p (c t) -> p c t", t=C)

            with scope():
                lor = {sg.name: sb("lor_" + sg.name, [128, 3, sg.T], BF16) for sg in rsegs}
                raws = [{sg.name: sb("raw%d_%s" % (i, sg.name), [128, sg.T + 1]) for sg in rsegs} for i in range(2)]
                t1s = [{sg.name: sb("sht%d_%s" % (i, sg.name), [128, sg.T]) for sg in rsegs} for i in range(2)]
                rawi = [0]
                rw_order = [0, 1, 2] + [3 + hp_ * 3 + j_ for hp_ in range(8) for j_ in range(3) if (need_r or j_ != 0)]
                rwq = Prefetch(wsl, [w_rw[t_] for t_ in rw_order])
                rw_pos = [0]

                def rw_inproj(tidx, dst):
                    assert rw_order[rw_pos[0]] == tidx
                    rw_pos[0] += 1
                    wt = rwq.next()
                    k_ = rawi[0] % 2
                    rawi[0] += 1
                    pbs = {}
                    for sg0 in segs:
                        pb = bank()
                        T0_ = sg0.T

                        def mm(e, pb=pb, wt=wt, sg0=sg0, T0_=T0_):
                            r_ = None
                            for kc in range(16):
                                r_ = e.matmul(pb[:, 0:T0_], wt[:, kc, :], hT[:, kc, sg0.off:sg0.off + T0_], start=(kc == 0), stop=(kc == 15))
                            return r_
                        op("pe", mm, r=[hT, wt], w=[pb])
                        for sg in rsegs:
                            if sg.off >= sg0.off and sg.off < sg0.off + T0_:
                                pbs[sg.name] = (pb, sg.off - sg0.off)
                    for sg in rsegs:
                        pb, po_ = pbs[sg.name]
                        T_ = sg.T
                        raw = raws[k_][sg.name]
                        t1 = t1s[k_][sg.name]
                        d_ = dst[sg.name]
                        op("act", lambda e: e.copy(raw[:, 1:1 + T_], pb[:, po_:po_ + T_]), r=[pb], w=[raw])
                        op("dve", lambda e: e.tensor_copy(raw[:, 0:1], sg.carry[:, tidx:tidx + 1]), r=[sg.carry], w=[raw])
                        op("dve", lambda e: e.tensor_copy(sg.carry[:, tidx:tidx + 1], raw[:, T_:T_ + 1]), r=[raw], w=[sg.carry])
                        op("act", lambda e: e.activation(t1[:, 0:T_], raw[:, 0:T_], AF.Identity, scale=vec[:, V_MU + tidx:V_MU + tidx + 1]), r=[raw, vec], w=[t1])
                        op("dve", lambda e: e.scalar_tensor_tensor(d_[:, 0:T_], raw[:, 1:1 + T_], omu[:, tidx:tidx + 1], t1[:, 0:T_], ALU.mult, ALU.add),
                           r=[raw, t1, omu], w=[d_])

                lt = {sg.name: sb("lorf_" + sg.name, [128, sg.T]) for sg in rsegs}
                rw_inproj(0, lt)
                for sg in rsegs:
                    op("act", lambda e: e.activation(lor[sg.name][0:64, 0, :], lt[sg.name][0:64, :], AF.Tanh), r=[lt[sg.name]], w=[lor[sg.name]])
                    op("dve", lambda e: e.tensor_copy(lor[sg.name][64:128, 0, :], lt[sg.name][64:128, :]), r=[lt[sg.name]], w=[lor[sg.name]])
                rw_inproj(1, lt)
                for sg in rsegs:
                    op("act", lambda e: e.activation(lor[sg.name][:, 1, :], lt[sg.name][:], AF.Sigmoid), r=[lt[sg.name]], w=[lor[sg.name]])
                rw_inproj(2, lt)
                for sg in rsegs:
                    op("act", lambda e: e.activation(lor[sg.name][0:32, 2, :], lt[sg.name][0:32, :], AF.Sigmoid), r=[lt[sg.name]], w=[lor[sg.name]])

                I2b = cb[:, 320:384]
                HS = 4
                TMPN = ("xr", "xk", "xv", "sig", "cum", "av", "Pt", "iP", "Pp", "Eh", "tmp", "tmp2", "tmp3", "kkt", "kmod", "bvec")
                for hset in range(8 // HS):
                    with scope():
                        hps = list(range(hset * HS, (hset + 1) * HS))
                        hsl = slice(hset * HS, (hset + 1) * HS)

                        def per(nm, shape_fn, dt=BF16):
                            return [{sg.name: sb("%s%d_%s" % (nm, i, sg.name), shape_fn(sg), dt) for sg in rsegs} for i in range(HS)]
                        AR = per("AR", lambda sg: [128, sg.nch, 2, 64])
                        W4 = per("W4", lambda sg: [128, sg.nch, 2, 64])
                        K4 = per("K4", lambda sg: [128, sg.nch, 2, 64])
                        Tt = per("Tt", lambda sg: [128, sg.nch, 64])
                        VT = per("VT", lambda sg: [128, sg.nch, 3, 64])
                        gv = per("gv", lambda sg: [128, sg.T])
                        bon = per("bon", lambda sg: [128, sg.T])
                        PC = {sg.name: sb("PC_" + sg.name, [128, sg.nch, HS]) for sg in rsegs}
                        Osb = {sg.name: sb("Osb_" + sg.name, [128, HS, sg.T]) for sg in rsegs}
                        with scope():
                            tm = {sg.name: {nm: sb("%s_%s" % (nm, sg.name), [128, sg.T]) for nm in TMPN} for sg in rsegs}
                            xb2 = [tm, tm]
                            BK2 = [{sg.name: sb("BK%d_%s" % (i, sg.name), [128, sg.nch, 2, 64], BF16) for sg in rsegs} for i in range(2)]
                            HAT2 = [{sg.name: sb("HAT%d_%s" % (i, sg.name), [128, sg.nch, 3, 64], BF16) for sg in rsegs} for i in range(2)]
                            Lc2 = [[{sg.name: sb("Lc%d%d_%s" % (j, i, sg.name), [128, sg.nch, 64], BF16) for sg in rsegs} for i in range(2)] for j in range(2)]
                            Wc2 = [[{sg.name: sb("Wc%d%d_%s" % (j, i, sg.name), [128, sg.nch, 64], BF16) for sg in rsegs} for i in range(2)] for j in range(2)]

                            def prep(hi, hp, sg, part):
                                T_, C, nch = sg.T, sg.C, sg.nch
                                n_ = sg.name
                                t = tm[n_]
                                BK, HAT, Lc, Wc = BK2[hi % 2], HAT2[hi % 2], Lc2[hi % 2], Wc2[hi % 2]
                                xs_ = xb2[hi % 2][n_]
                                xr, xk, xv, sig, cum, av = xs_["xr"], xs_["xk"], xs_["xv"], t["sig"], t["cum"], t["av"]
                                Pt, iP, Pp, Eh, tmp, tmp2, tmp3 = t["Pt"], t["iP"], t["Pp"], t["Eh"], t["tmp"], t["tmp2"], t["tmp3"]
                                kkt, kmod, bvec = t["kkt"], t["kmod"], t["bvec"]
                                lr = lor[n_]
                                ARh, W4h, K4h, Tth, VTh, BKs, HATs = AR[hi][n_], W4[hi][n_], K4[hi][n_], Tt[hi][n_], VT[hi][n_], BK[n_], HAT[n_]
                                hc = slice(hp * 128, (hp + 1) * 128)
                                if part == "B":
                                    yield from prepB(hi, hp, sg, ARh, W4h, K4h, Tth, VTh, BKs, HATs, Lc, Wc)
                                    return
                                pw = bank()
                                op("pe", lambda e: e.matmul(pw[:, 0:T_], lwa[0:64, hc], lr[0:64, 0, :], start=True, stop=True, tile_position=(0, 0)), r=[lwa, lr], w=[pw])
                                op("act", lambda e: e.activation(sig[:], pw[:, 0:T_], AF.Sigmoid, bias=vec[:, V_W0 + hp:V_W0 + hp + 1]), r=[pw, vec], w=[sig])
                                yield
                                pa = bank()
                                op("pe", lambda e: e.matmul(pa[:, 0:T_], lwa[64:128, hc], lr[64:128, 0, :], start=True, stop=True, tile_position=(64, 0)), r=[lwa, lr], w=[pa])
                                op("act", lambda e: e.activation(av[:], pa[:, 0:T_], AF.Sigmoid, bias=vec[:, V_A0 + hp:V_A0 + hp + 1]), r=[pa, vec], w=[av])
                                yield
                                if sg.out:
                                    pg = bank()

                                    def mmg(e):
                                        e.matmul(pg[:, 0:T_], lg[:, hc], lr[:, 1, :], start=True, stop=False)
                                        return e.matmul(pg[:, 0:T_], lgb[0:32, hc], lr[0:32, 2, :], start=False, stop=True)
                                    op("pe", mmg, r=[lg, lgb, lr], w=[pg])
                                    op("act", lambda e: e.copy(gv[hi][n_][:], pg[:, 0:T_]), r=[pg], w=[gv[hi][n_]])
                                yield
                                op("dve", lambda e: e.tensor_tensor_scan(cum[:], rmask[:, 0:T_], sig[:], 0.0, ALU.mult, ALU.add), r=[sig, cst], w=[cum])
                                yield
                                op("act", lambda e: e.activation(Pt[:], cum[:], AF.Exp, scale=-C0), r=[cum], w=[Pt])
                                op("act", lambda e: e.activation(iP[:], cum[:], AF.Exp, scale=C0), r=[cum], w=[iP])
                                op("pool", lambda e: e.tensor_tensor(tmp[:], cum[:], sig[:], ALU.subtract), r=[cum, sig], w=[tmp])
                                yield
                                op("act", lambda e: e.activation(Pp[:], tmp[:], AF.Exp, scale=-C0), r=[tmp], w=[Pp])
                                cum3 = v3(cum[:], C)
                                op("dve", lambda e: e.tensor_tensor(v3(tmp2[:], C), cum3, cum3[:, :, C - 1:C].to_broadcast([128, nch, C]), ALU.subtract), r=[cum], w=[tmp2])
                                yield
                                op("act", lambda e: e.activation(Eh[:], tmp2[:], AF.Exp, scale=C0), r=[tmp2], w=[Eh])
                                op("act", lambda e: e.activation(PC[n_][:, :, hi:hi + 1], cum3[:, :, C - 1:C], AF.Exp, scale=-C0), r=[cum], w=[PC[n_]])
                                op("act", lambda e: e.activation(tmp3[:], xk[:], AF.Square, scale=vec[:, V_KK + hp:V_KK + hp + 1]), r=[xk, vec], w=[tmp3])
                                yield
                                pn = bank()
                                op("pe", lambda e: e.matmul(pn[:, 0:T_], bones, tmp3[:], start=True, stop=True), r=[cst, tmp3], w=[pn])
                                op("act", lambda e: e.activation(tmp3[:], pn[:, 0:T_], AF.Ln, bias=1e-18, scale=1.0), r=[pn], w=[tmp3])
                                yield
                                op("act", lambda e: e.activation(tmp3[:], tmp3[:], AF.Exp, scale=-0.5), r=[tmp3], w=[tmp3])
                                op("act", lambda e: e.activation(kkt[:], xk[:], AF.Identity, scale=vec[:, V_KK + hp:V_KK + hp + 1]), r=[xk, vec], w=[kkt])
                                op("pool", lambda e: e.tensor_tensor(kkt[:], kkt[:], tmp3[:], ALU.mult), r=[kkt, tmp3], w=[kkt])
                                yield
                                op("act", lambda e: e.activation(tmp[:], av[:], AF.Identity, scale=vec[:, V_KA + hp:V_KA + hp + 1], bias=nka[:, hp:hp + 1]), r=[av, vec, nka], w=[tmp])
                                op("dve", lambda e: e.scalar_tensor_tensor(kmod[:], tmp[:], 1.0, xk[:], ALU.add, ALU.mult), r=[tmp, xk], w=[kmod])
                                yield
                                op("pool", lambda e: e.tensor_tensor(bvec[:], kkt[:], av[:], ALU.mult), r=[kkt, av], w=[bvec])
                                op("dve", lambda e: e.scalar_tensor_tensor(ARh[:, :, 0, 0:C], v3(kkt[:], C), -1.0, v3(Pp[:], C), ALU.mult, ALU.mult), r=[kkt, Pp], w=[ARh])
                                yield
                                if sg.out:
                                    op("pool", lambda e: e.tensor_tensor(ARh[:, :, 1, 0:C], v3(xr[:], C), v3(Pt[:], C), ALU.mult), r=[xr, Pt], w=[ARh])
                                op("pool", lambda e: e.tensor_tensor(BKs[:, :, 0, 0:C], v3(bvec[:], C), v3(iP[:], C), ALU.mult), r=[bvec, iP], w=[BKs])
                                yield
                                op("pool", lambda e: e.tensor_tensor(BKs[:, :, 1, 0:C], v3(kmod[:], C), v3(iP[:], C), ALU.mult), r=[kmod, iP], w=[BKs])
                                op("pool", lambda e: e.tensor_tensor(HATs[:, :, 0, 0:C], v3(bvec[:], C), v3(Eh[:], C), ALU.mult), r=[bvec, Eh], w=[HATs])
                                yield
                                op("pool", lambda e: e.tensor_tensor(HATs[:, :, 1, 0:C], v3(kmod[:], C), v3(Eh[:], C), ALU.mult), r=[kmod, Eh], w=[HATs])
                                op("act", lambda e: e.copy(HATs[:, :, 2, 0:C], v3(xv[:], C)), r=[xv], w=[HATs])
                                if sg.out:
                                    op("dve", lambda e: e.scalar_tensor_tensor(tmp[:], xr[:], vec[:, V_RK + hp:V_RK + hp + 1], kmod[:], ALU.mult, ALU.mult), r=[xr, vec, kmod], w=[tmp])
                                    yield
                                    pbn = bank()
                                    op("pe", lambda e: e.matmul(pbn[:, 0:T_], bones, tmp[:], start=True, stop=True), r=[cst, tmp], w=[pbn])
                                    op("dve", lambda e: e.tensor_tensor(bon[hi][n_][:], pbn[:, 0:T_], xv[:], ALU.mult), r=[pbn, xv], w=[bon[hi][n_]])
                                yield
                                return

                            def prepB(hi, hp, sg, ARh, W4h, K4h, Tth, VTh, BKs, HATs, Lc, Wc):
                                T_, C, nch = sg.T, sg.C, sg.nch
                                n_ = sg.name
                                pL = bank()

                                def mmL(e):
                                    r_ = None
                                    for c in range(nch):
                                        for e_ in range(2):
                                            ps_ = slice(e_ * 64, e_ * 64 + 64)
                                            r_ = e.matmul(pL[e_ * 64:e_ * 64 + C, c * 64:c * 64 + C], ARh[ps_, c, 0, 0:C], BKs[ps_, c, 0, 0:C],
                                                          start=True, stop=True, tile_position=(e_ * 64, e_ * 64))
                                    return r_
                                op("pe", mmL, r=[ARh, BKs], w=[pL])
                                L0 = Lc[0][n_]
                                W0 = Wc[0][n_]
                                op("dve", lambda e: e.tensor_tensor(L0[:, :, 0:C], v3(pL[:, 0:nch * 64], 64)[:, :, 0:C],
                                                                     mL[:, 0:C].unsqueeze(1).to_broadcast([128, nch, C]), ALU.mult), r=[pL, cst], w=[L0])
                                yield
                                mk = mUI.rearrange("p (x t) -> p x t", t=64)
                                for (srcBK, dst4) in ((0, W4h), (1, K4h)):
                                    pW = bank()

                                    def mmW(e, pW=pW, srcBK=srcBK):
                                        r_ = None
                                        for c in range(nch):
                                            for e_ in range(2):
                                                ps_ = slice(e_ * 64, e_ * 64 + 64)
                                                if sg.out:
                                                    r_ = e.matmul(pW[e_ * 64:e_ * 64 + C, c * 128:c * 128 + 2 * C].rearrange("p (x t) -> p x t", t=C),
                                                                  BKs[ps_, c, srcBK, 0:C], ARh[ps_, c, :, 0:C], start=True, stop=True, tile_position=(e_ * 64, e_ * 64))
                                                else:
                                                    r_ = e.matmul(pW[e_ * 64:e_ * 64 + C, c * 128:c * 128 + C],
                                                                  BKs[ps_, c, srcBK, 0:C], ARh[ps_, c, 0, 0:C], start=True, stop=True, tile_position=(e_ * 64, e_ * 64))
                                        return r_
                                    op("pe", mmW, r=[ARh, BKs], w=[pW])
                                    if sg.out and C == 64:
                                        src = pW[:, 0:nch * 128].rearrange("p (c x t) -> p c x t", x=2, t=64)
                                        op("dve", lambda e, src=src, dst4=dst4: e.tensor_tensor(
                                            dst4[:, :, :, :], src, mk.unsqueeze(1).to_broadcast([128, nch, 2, 64]), ALU.mult), r=[pW, cst], w=[dst4])
                                    else:
                                        for x in range(2 if sg.out else 1):
                                            if C == 64:
                                                src = pW[:, 0:nch * 128].rearrange("p (c x t) -> p c x t", x=2, t=64)[:, :, x, :]
                                            else:
                                                src = pW[:, 0:2 * C].rearrange("p (c x t) -> p c x t", c=1, x=2, t=C)[:, :, x, :]
                                            op("dve", lambda e, src=src, x=x, dst4=dst4: e.tensor_tensor(
                                                dst4[:, :, x, 0:C], src, mk[:, x, 0:C].unsqueeze(1).to_broadcast([128, nch, C]), ALU.mult), r=[pW, cst], w=[dst4])
                                    yield
                                op("dve", lambda e: e.tensor_tensor(Tth[:, :, 0:C], W4h[:, :, 0, 0:C], I2b[:, 0:C].unsqueeze(1).to_broadcast([128, nch, C]), ALU.add),
                                   r=[W4h, cb], w=[Tth])
                                op("act", lambda e: e.copy(W0[:, :, 0:C], W4h[:, :, 0, 0:C]), r=[W4h], w=[W0])
                                yield
                                nlev = 5 if C == 64 else 4

                                def mmsq(e, pX, A, B):
                                    r_ = None
                                    for c in range(nch):
                                        for e_ in range(2):
                                            rs = slice(e_ * 64, e_ * 64 + C)
                                            r_ = e.matmul(pX[rs, c * 64:c * 64 + C], A[rs, c, 0:C], B[rs, c, 0:C], start=True, stop=True,
                                                          tile_position=(e_ * 64, e_ * 64))
                                    return r_
                                for lv in range(1, nlev + 1):
                                    Lo, Wo = Lc[(lv - 1) % 2][n_], Wc[(lv - 1) % 2][n_]
                                    Ln_, Wn = Lc[lv % 2][n_], Wc[lv % 2][n_]
                                    pL2 = bank()
                                    op("pe", lambda e, pL2=pL2, Lo=Lo, Wo=Wo: mmsq(e, pL2, Wo, Lo), r=[Wo, Lo], w=[pL2])
                                    if lv % 2 == 1:
                                        op("act", lambda e, pL2=pL2, Ln_=Ln_: e.copy(Ln_[:, :, 0:C], v3(pL2[:, 0:nch * 64], 64)[:, :, 0:C]), r=[pL2], w=[Ln_])
                                    else:
                                        op("dve", lambda e, pL2=pL2, Ln_=Ln_: e.tensor_copy(Ln_[:, :, 0:C], v3(pL2[:, 0:nch * 64], 64)[:, :, 0:C]), r=[pL2], w=[Ln_])
                                    if lv < nlev:
                                        pW2 = bank()
                                        op("pe", lambda e, pW2=pW2, Lo=Lo, Wo=Wo: mmsq(e, pW2, Lo, Wo), r=[Wo, Lo], w=[pW2])
                                        if lv % 2 == 1:
                                            op("dve", lambda e, pW2=pW2, Wn=Wn: e.tensor_copy(Wn[:, :, 0:C], v3(pW2[:, 0:nch * 64], 64)[:, :, 0:C]), r=[pW2], w=[Wn])
                                        else:
                                            op("act", lambda e, pW2=pW2, Wn=Wn: e.copy(Wn[:, :, 0:C], v3(pW2[:, 0:nch * 64], 64)[:, :, 0:C]), r=[pW2], w=[Wn])
                                    yield
                                    pT_ = bank()
                                    op("pe", lambda e, pT_=pT_, Ln_=Ln_: mmsq(e, pT_, Ln_, Tth), r=[Ln_, Tth], w=[pT_])
                                    op("dve", lambda e, pT_=pT_: e.tensor_tensor(Tth[:, :, 0:C], v3(pT_[:, 0:nch * 64], 64)[:, :, 0:C], Tth[:, :, 0:C], ALU.add),
                                       r=[pT_, Tth], w=[Tth])
                                    yield
                                for x in range(3):
                                    pV = bank()

                                    def mmV(e, pV=pV, x=x):
                                        r_ = None
                                        for c in range(nch):
                                            for e_ in range(2):
                                                ps_ = slice(e_ * 64, e_ * 64 + 64)
                                                r_ = e.matmul(pV[e_ * 64:e_ * 64 + C, c * 64:c * 64 + 64], HATs[ps_, c, x, 0:C], identb[ps_, e_ * 64:e_ * 64 + 64],
                                                              start=True, stop=True, tile_position=(e_ * 64, e_ * 64))
                                        return r_
                                    op("pe", mmV, r=[HATs, cb], w=[pV])
                                    if x == 1:
                                        op("act", lambda e, pV=pV, x=x: e.copy(VTh[:, :, x, :], v3(pV[:, 0:nch * 64], 64)), r=[pV], w=[VTh])
                                    else:
                                        op("act", lambda e, pV=pV, x=x: e.copy(VTh[:, :, x, :], v3(pV[:, 0:nch * 64], 64)), r=[pV], w=[VTh])
                                    yield

                            def inproj3(hi, hp):
                                xx = xb2[hi % 2]
                                if need_r:
                                    rw_inproj(3 + hp * 3 + 0, {n_: xx[n_]["xr"] for n_ in xx})
                                rw_inproj(3 + hp * 3 + 1, {n_: xx[n_]["xk"] for n_ in xx})
                                rw_inproj(3 + hp * 3 + 2, {n_: xx[n_]["xv"] for n_ in xx})
                            def stageA(hi, hp):
                                inproj3(hi, hp)
                                yield
                                gens = [prep(hi, hp, sg, "A") for sg in rsegs]
                                while gens:
                                    for g_ in list(gens):
                                        try:
                                            next(g_)
                                        except StopIteration:
                                            gens.remove(g_)
                                    yield
                            rr([stageA(0, hps[0])])
                            for hi, hp in enumerate(hps):
                                gl = [prep(hi, hp, sg, "B") for sg in rsegs]
                                if hi + 1 < HS:
                                    gl = [stageA(hi + 1, hps[hi + 1])] + gl
                                rr(gl)
                        Sb = [sb("Sb0", [128, HS, 64], BF16), sb("Sb1", [128, HS, 64], BF16)]
                        Xb = sb("Xb", [128, HS, 64], BF16)
                        Ub = sb("Ub", [128, HS, 64], BF16)
                        for sg in rsegs:
                            C, nch, n_ = sg.C, sg.nch, sg.name
                            S32 = sg.S32
                            par = 0
                            ARs = [AR[hi][n_] for hi in range(HS)]
                            W4s = [W4[hi][n_] for hi in range(HS)]
                            K4s = [K4[hi][n_] for hi in range(HS)]
                            Tts = [Tt[hi][n_] for hi in range(HS)]
                            VTs = [VT[hi][n_] for hi in range(HS)]
                            op("act", lambda e: e.copy(Sb[0][:], S32[:, hsl, :]), r=[S32], w=[Sb[0]])
                            for c in range(nch):
                                Sc, Sn = Sb[par], Sb[1 - par]
                                par = 1 - par
                                pX = bank()

                                def mmX(e):
                                    r_ = None
                                    for hi in range(HS):
                                        for e_ in range(2):
                                            ps_ = slice(e_ * 64, e_ * 64 + 64)
                                            rs = slice(e_ * 64, e_ * 64 + C)
                                            e.matmul(pX[rs, hi * 64:hi * 64 + 64], ARs[hi][ps_, c, 0, 0:C], Sc[ps_, hi, :], start=True, stop=False,
                                                     tile_position=(e_ * 64, e_ * 64))
                                            r_ = e.matmul(pX[rs, hi * 64:hi * 64 + 64], K4s[hi][rs, c, 0, 0:C], VTs[hi][rs, c, 2, :], start=False, stop=True,
                                                          tile_position=(e_ * 64, e_ * 64))
                                    return r_
                                op("pe", mmX, r=ARs + K4s + VTs + [Sc], w=[pX])
                                op("act", lambda e: e.copy(Xb[:].rearrange("p h i -> p (h i)"), pX[:, 0:HS * 64]), r=[pX], w=[Xb])
                                op("dve", lambda e: e.tensor_tensor(S32[:, hsl, :], S32[:, hsl, :], PC[n_][:, c, :].unsqueeze(2).to_broadcast([128, HS, 64]), ALU.mult),
                                   r=[S32, PC[n_]], w=[S32])
                                pU = bank()

                                def mmU(e):
                                    r_ = None
                                    for hi in range(HS):
                                        for e_ in range(2):
                                            rs = slice(e_ * 64, e_ * 64 + C)
                                            r_ = e.matmul(pU[rs, hi * 64:hi * 64 + 64], Tts[hi][rs, c, 0:C], Xb[rs, hi, :], start=True, stop=True,
                                                          tile_position=(e_ * 64, e_ * 64))
                                    return r_
                                op("pe", mmU, r=Tts + [Xb], w=[pU])
                                op("dve", lambda e: e.tensor_copy(Ub[:].rearrange("p h i -> p (h i)"), pU[:, 0:HS * 64]), r=[pU], w=[Ub])
                                pS = bank()

                                def mmS(e):
                                    r_ = None
                                    for hi in range(HS):
                                        for e_ in range(2):
                                            ps_ = slice(e_ * 64, e_ * 64 + 64)
                                            rs = slice(e_ * 64, e_ * 64 + C)
                                            e.matmul(pS[ps_, hi * 64:hi * 64 + 64], VTs[hi][rs, c, 0, :], Ub[rs, hi, :], start=True, stop=False,
                                                     tile_position=(e_ * 64, e_ * 64))
                                            r_ = e.matmul(pS[ps_, hi * 64:hi * 64 + 64], VTs[hi][rs, c, 1, :], VTs[hi][rs, c, 2, :], start=False, stop=True,
                                                          tile_position=(e_ * 64, e_ * 64))
                                    return r_
                                op("pe", mmS, r=VTs + [Ub], w=[pS])
                                op("dve", lambda e: e.tensor_tensor(S32[:, hsl, :], v3(pS[:, 0:HS * 64], 64), S32[:, hsl, :], ALU.add), r=[S32, pS], w=[S32])
                                if c < nch - 1:
                                    op("act", lambda e: e.copy(Sn[:], S32[:, hsl, :]), r=[S32], w=[Sn])
                                pO = bank()

                                def mmO(e):
                                    r_ = None
                                    for hi in range(HS):
                                        for e_ in range(2):
                                            ps_ = slice(e_ * 64, e_ * 64 + 64)
                                            rs = slice(e_ * 64, e_ * 64 + C)
                                            e.matmul(pO[ps_, hi * 64:hi * 64 + C], Sc[ps_, hi, :], ARs[hi][ps_, c, 1, 0:C], start=True, stop=False,
                                                     tile_position=(e_ * 64, e_ * 64))
                                            e.matmul(pO[ps_, hi * 64:hi * 64 + C], Ub[rs, hi, :], W4s[hi][rs, c, 1, 0:C], start=False, stop=False,
                                                     tile_position=(e_ * 64, e_ * 64))
                                            r_ = e.matmul(pO[ps_, hi * 64:hi * 64 + C], VTs[hi][rs, c, 2, :], K4s[hi][rs, c, 1, 0:C], start=False, stop=True,
                                                          tile_position=(e_ * 64, e_ * 64))
                                    return r_
                                if sg.out:
                                    op("pe", mmO, r=ARs + W4s + K4s + VTs + [Sc, Ub], w=[pO])
                                    op("act", lambda e: e.copy(Osb[n_][:, :, c * C:(c + 1) * C], v3(pO[:, 0:HS * 64], 64)[:, :, 0:C]), r=[pO], w=[Osb[n_]])
                        with scope():
                            ptm = [{sg.name: {nm: sb("%s%d_%s" % (nm, i, sg.name), [128, sg.T]) for nm in ("osq", "msb", "tq")} for sg in rsegs} for i in range(HS)]

                            def post(hi, hp, sg):
                                T_, n_ = sg.T, sg.name
                                osq, msb, tq = ptm[hi][n_]["osq"], ptm[hi][n_]["msb"], ptm[hi][n_]["tq"]
                                O_ = Osb[n_]
                                cs = slice(sg.off, sg.off + T_)
                                pm = bank()
                                op("pe", lambda e: e.matmul(pm[:, 0:T_], bones64[:], O_[:, hi, :], start=True, stop=True), r=[bones64, O_], w=[pm])
                                op("act", lambda e: e.copy(msb[:], pm[:, 0:T_]), r=[pm], w=[msb])
                                yield
                                op("act", lambda e: e.activation(osq[:], O_[:, hi, :], AF.Square), r=[O_], w=[osq])
                                op("dve", lambda e: e.tensor_tensor(tq[:], msb[:], msb[:], ALU.mult), r=[msb], w=[tq])
                                yield
                                pq = bank()
                                op("pe", lambda e: e.matmul(pq[:, 0:T_], bones64[:], osq[:], start=True, stop=True), r=[bones64, osq], w=[pq])
                                op("dve", lambda e: e.tensor_tensor(tq[:], pq[:, 0:T_], tq[:], ALU.subtract), r=[pq, tq], w=[tq])
                                yield
                                op("act", lambda e: e.activation(tq[:], tq[:], AF.Ln, bias=LNX_EPS, scale=1.0), r=[tq], w=[tq])
                                op("pool", lambda e: e.tensor_tensor(osq[:], O_[:, hi, :], msb[:], ALU.subtract), r=[O_, msb], w=[osq])
                                yield
                                op("act", lambda e: e.activation(tq[:], tq[:], AF.Exp, scale=-0.5), r=[tq], w=[tq])
                                op("dve", lambda e: e.tensor_tensor(osq[:], osq[:], tq[:], ALU.mult), r=[osq, tq], w=[osq])
                                yield
                                op("act", lambda e: e.activation(osq[:], osq[:], AF.Identity, scale=vec[:, V_XG + hp:V_XG + hp + 1], bias=vec[:, V_XB + hp:V_XB + hp + 1]),
                                   r=[osq, vec], w=[osq])
                                op("dve", lambda e: e.tensor_tensor(osq[:], osq[:], bon[hi][n_][:], ALU.add), r=[osq, bon[hi][n_]], w=[osq])
                                yield
                                op("pool", lambda e: e.tensor_tensor(mixT[:, 8 + hp, cs], osq[:], gv[hi][n_][:], ALU.mult), r=[osq, gv[hi][n_]], w=[mixT])
                                yield
                            rr([post(hi, hp, sg) for hi, hp in enumerate(hps) for sg in rsegs if sg.out])
            dump("mix%d" % gi, mixT, mixT[:], [128, 16, TG])
            if gi == NG - 1:
                dma("sp", pwkv_o.rearrange("h p i -> p h i"), S32_p[:], r=[S32_p], final=True)
                dma("sp", swkv_o.rearrange("h p i -> p h i"), S32_s[:], r=[S32_s], final=True)
                dma("sp", pshift_o, carry_p[:], r=[carry_p], final=True)
                dma("sp", sshift_o, carry_s[:], r=[carry_s], final=True)
            if stop == "rw":
                continue
            if gi == 0:
                continue
            with scope():
                h1T = hT
                post_tiles = []
                n = 0
                for sg in segs:
                    for ti, rows in enumerate(sg.rows):
                        if gi >= 2 or (sg.name == "p" and ti == 3):
                            post_tiles.append((sg, ti, rows, n))
                        n += 1
                ntl = n
                with scope():
                    Z = sb("Z", [128, ntl, D])
                    wos = Prefetch(Slots("wos", [128, 16, 512], 2), [w_outd[b_] for b_ in range(4)])
                    for blk in range(4):
                        wt = wos.next()
                        for (sg, ti, rows, idx) in post_tiles:
                            c0 = sg.off + ti * 128
                            pb = bank()

                            def mm(e, pb=pb, wt=wt, c0=c0, rows=rows, blk=blk):
                                r_ = None
                                for kc in range(16):
                                    e.matmul(pb[:rows, 0:512], mixT[:, kc, c0:c0 + rows], wt[:, kc, :], start=(kc == 0), stop=False)
                                for j in range(4):
                                    r_ = e.matmul(pb[:rows, j * 128:(j + 1) * 128], hT[:, blk * 4 + j, c0:c0 + rows], aidentb, start=False, stop=(j == 3))
                                return r_
                            op("pe", mm, r=[mixT, hT, wt, cb], w=[pb])
                            if (idx + blk) % 2 == 0:
                                op("act", lambda e, pb=pb, idx=idx, rows=rows, blk=blk: e.copy(Z[:rows, idx, blk * 512:(blk + 1) * 512], pb[:rows, 0:512]), r=[pb], w=[Z])
                            else:
                                op("dve", lambda e, pb=pb, idx=idx, rows=rows, blk=blk: e.tensor_copy(Z[:rows, idx, blk * 512:(blk + 1) * 512], pb[:rows, 0:512]), r=[pb], w=[Z])
                    xnbs = [sb("xnb1_0", [128, D], BF16), sb("xnb1_1", [128, D], BF16)]
                    pend = [None]
                    for k_, (sg, ti, rows, idx) in enumerate(post_tiles):
                        zt = T(Z.h[:, idx, :], Z.b)
                        mv = ln_stats(zt, rows, "l1")
                        xnb = xnbs[k_ % 2]
                        op("act", lambda e, zt=zt, xnb=xnb, mv=mv, rows=rows: e.activation(
                            xnb[:rows, :], zt[:rows, :], AF.Identity, bias=mv[:rows, 3:4], scale=mv[:rows, 2:3]), r=[Z, mv], w=[xnb])
                        fm = (lambda xnb=xnb, rows=rows, c0=sg.off + ti * 128: to_feature_major(xnb, rows, h1T, c0, V_L1G, V_L1B))
                        if pend[0] is not None:
                            pend[0]()
                        pend[0] = fm
                    if pend[0] is not None:
                        pend[0]()
                dump("h1T%d" % gi, h1T, h1T[:], [128, 16, TG])
                if gi == 1:
                    op("dve", lambda e: e.tensor_copy(h1halo[:], h1T[:, :, GT - 2:GT]), r=[h1T], w=[h1halo])
                    continue
                if stop == "ln1":
                    continue
                with scope():
                    R = sb("R", [128, ntl, D])
                    g2bc = sb("g2bc", [128, D])
                    b2bc = sb("b2bc", [128, D])
                    dma("sp", g2bc[:], ln2g_d.partition_broadcast(128), w=[g2bc])
                    dma("sp", b2bc[:], ln2b_d.partition_broadcast(128), w=[b2bc])
                    actT = [sb("actT0", [128, 4, TG], BF16), sb("actT1", [128, 4, TG], BF16)]
                    wds = Prefetch(Slots("wds", [128, 4, 2048], 2), [w_dn[p_] for p_ in range(11)])
                    upq = Prefetch(wsl, [w_up[2 * (4 * p_ + j_) + g_] for p_ in range(11) for j_ in range(4) for g_ in range(2)])
                    Us = [sb("Ub%d" % i, [128, TG + 4]) for i in range(4)]
                    accs = [sb("acc%d" % i, [128, TG]) for i in range(4)]

                    def segviews(t_):
                        d_ = {}
                        for sg_ in segs:
                            b_ = Buf(t_.b.name + "_" + sg_.name, grave)
                            scope_bufs[-1].append(b_)
                            d_[sg_.name] = T(t_.h, b_)
                        return d_
                    Useg = [segviews(t_) for t_ in Us]
                    aseg = [segviews(t_) for t_ in accs]
                    if gi == NG - 1:
                        build.sbuf_left_ffn = nc.sbuf_bytes_remaining
                    for si_, sg in enumerate(segs):
                        sg.uoff = sg.off + 2 * si_
                        sg.cvc = cvc_p if sg.name == "p" else cvc_s
                    def emit_up(pi):
                        aT = actT[pi % 2]
                        for j in range(4):
                            for gvx in range(2):
                                ft = 2 * (4 * pi + j) + gvx
                                wt = upq.next()
                                if gi == 2:
                                    ph = bank()

                                    def mmh(e, ph=ph, wt=wt):
                                        r_ = None
                                        for kc in range(16):
                                            r_ = e.matmul(ph[:, 0:2], wt[:, kc, :], h1halo[:, kc, :], start=(kc == 0), stop=(kc == 15))
                                        return r_
                                    op("pe", mmh, r=[wt, h1halo], w=[ph])
                                    op("dve", lambda e, ph=ph, ft=ft: e.tensor_scalar(cvc_p[:, ft, :], ph[:, 0:2], flag[:, 0:1], None, ALU.mult), r=[ph, flag], w=[cvc_p])
                                for sg in segs:
                                    T_ = sg.T
                                    pb = bank()

                                    def mm(e, pb=pb, wt=wt, sg=sg, T_=T_):
                                        r_ = None
                                        for kc in range(16):
                                            r_ = e.matmul(pb[:, 0:T_], wt[:, kc, :], h1T[:, kc, sg.off:sg.off + T_], start=(kc == 0), stop=(kc == 15))
                                        return r_
                                    op("pe", mm, r=[wt, h1T], w=[pb])
                                    uo = sg.uoff
                                    U = Useg[(j % 2) * 2 + gvx][sg.name]
                                    acc = aseg[(j % 2) * 2 + gvx][sg.name]
                                    cs = slice(sg.off, sg.off + T_)
                                    op("act", lambda e, pb=pb, uo=uo, T_=T_, U=U: e.copy(U[:, uo + 2:uo + 2 + T_], pb[:, 0:T_]), r=[pb], w=[U])
                                    op("dve", lambda e, uo=uo, U=U, sg=sg, ft=ft: e.tensor_copy(U[:, uo:uo + 2], sg.cvc[:, ft, :]), r=[sg.cvc], w=[U])
                                    op("dve", lambda e, uo=uo, U=U, sg=sg, ft=ft, T_=T_: e.tensor_copy(sg.cvc[:, ft, :], U[:, uo + T_:uo + T_ + 2]), r=[U], w=[sg.cvc])
                                    op("act", lambda e, uo=uo, U=U, T_=T_, acc=acc, cs=cs, ft=ft: e.activation(
                                        acc[:, cs], U[:, uo + 2:uo + 2 + T_], AF.Identity, bias=cvb[:, ft:ft + 1], scale=cvw[:, ft, 2:3]), r=[U, cvb, cvw], w=[acc])
                                    op("dve", lambda e, uo=uo, U=U, T_=T_, acc=acc, cs=cs, ft=ft: e.scalar_tensor_tensor(
                                        acc[:, cs], U[:, uo + 1:uo + 1 + T_], cvw[:, ft, 1:2], acc[:, cs], ALU.mult, ALU.add), r=[U, cvw, acc], w=[acc])
                                    op("dve", lambda e, uo=uo, U=U, T_=T_, acc=acc, cs=cs, ft=ft: e.scalar_tensor_tensor(
                                        acc[:, cs], U[:, uo:uo + T_], cvw[:, ft, 0:1], acc[:, cs], ALU.mult, ALU.add), r=[U, cvw, acc], w=[acc])
                                    if gvx == 0:
                                        op("act", lambda e, acc=acc, cs=cs: e.activation(acc[:, cs], acc[:, cs], AF.Gelu), r=[acc], w=[acc])
                                    else:
                                        a0_, a1_ = aseg[(j % 2) * 2][sg.name], aseg[(j % 2) * 2 + 1][sg.name]
                                        op("dve", lambda e, cs=cs, aT=aT, j=j, a0_=a0_, a1_=a1_: e.tensor_tensor(aT[:, j, cs], a0_[:, cs], a1_[:, cs], ALU.mult),
                                           r=[a0_, a1_], w=[aT])

                    def emit_down(pi):
                        aT = actT[pi % 2]
                        wd = wds.next()
                        for (sg, ti, rows, idx) in post_tiles:
                            c0 = sg.off + ti * 128
                            for cb4 in range(4):
                                pb = bank()

                                def mmd(e, pb=pb, wd=wd, c0=c0, rows=rows, cb4=cb4, aT=aT, pi=pi):
                                    r_ = None
                                    for j in range(4):
                                        r_ = e.matmul(pb[:rows, 0:512], aT[:, j, c0:c0 + rows], wd[:, j, cb4 * 512:(cb4 + 1) * 512], start=(j == 0),
                                                      stop=(j == 3 and pi != 0))
                                    if pi == 0:
                                        for j in range(4):
                                            r_ = e.matmul(pb[:rows, j * 128:(j + 1) * 128], h1T[:, cb4 * 4 + j, c0:c0 + rows], aidentb, start=False, stop=(j == 3))
                                    return r_
                                op("pe", mmd, r=[aT, wd, h1T, cb], w=[pb])
                                if pi == 0:
                                    op("act", lambda e, pb=pb, idx=idx, rows=rows, cb4=cb4: e.copy(R[:rows, idx, cb4 * 512:(cb4 + 1) * 512], pb[:rows, 0:512]), r=[pb], w=[R])
                                else:
                                    op("dve", lambda e, pb=pb, idx=idx, rows=rows, cb4=cb4: e.tensor_tensor(
                                        R[:rows, idx, cb4 * 512:(cb4 + 1) * 512], pb[:rows, 0:512], R[:rows, idx, cb4 * 512:(cb4 + 1) * 512], ALU.add), r=[pb, R], w=[R])

                    emit_up(0)
                    for pi in range(11):
                        if pi + 1 < 11:
                            emit_up(pi + 1)
                        emit_down(pi)
                    rtiles = {}
                    for (sg, ti, rows, idx) in post_tiles:
                        b_ = Buf("Rt%d" % idx)
                        b_.w = R.b.w
                        b_.r = dict(R.b.r)
                        scope_bufs[-1].append(b_)
                        rtiles[idx] = T(R.h[:, idx, :], b_)
                    pend = [None]
                    for (sg, ti, rows, idx) in post_tiles:
                        rt = rtiles[idx]
                        mv = ln_stats(rt, rows, "l2")
                        op("act", lambda e, rt=rt, mv=mv, rows=rows: e.activation(
                            rt[:rows, :], rt[:rows, :], AF.Identity, bias=mv[:rows, 3:4], scale=mv[:rows, 2:3]), r=[rt, mv], w=[rt])

                        def tail(rt=rt, rows=rows, sg=sg, ti=ti):
                            op("dve", lambda e: e.tensor_tensor(rt[:rows, :], rt[:rows, :], g2bc[:rows, :], ALU.mult), r=[rt, g2bc], w=[rt])
                            op("dve", lambda e: e.tensor_tensor(rt[:rows, :], rt[:rows, :], b2bc[:rows, :], ALU.add), r=[rt, b2bc], w=[rt])
                            if sg.name == "p":
                                r0 = (gi - 2) * GT + ti * 128
                                dma("sp", y_main[r0:r0 + rows, :], rt[:rows, :], r=[rt], final=True)
                            else:
                                dma("sp", y_s[0:rows, :], rt[:rows, :], r=[rt], final=True)
                        if pend[0] is not None:
                            pend[0]()
                        pend[0] = tail
                    if pend[0] is not None:
                        pend[0]()
                    if gi == NG - 1:
                        dma("sp", pconv_o, cvc_p[:], r=[cvc_p], final=True)
                        dma("sp", sconv_o, cvc_s[:], r=[cvc_s], final=True)
    S.finish()
    root.close()
    build.last_ninst = dict(S.ninst)
    build.nsem = len(S.sems)
    return nc, dbg_out


def _TtView(t):
    return t


RW_R, RW_WD, RW_K, RW_V, RW_AD, RW_GD = 0, 1024, 1088, 2112, 3136, 3200


def rw_tile_cols():
    tiles = []
    tiles.append(np.concatenate([np.arange(RW_WD, RW_WD + 64), np.arange(RW_AD, RW_AD + 64)]))
    tiles.append(np.arange(RW_GD, RW_GD + 128))
    tiles.append(np.concatenate([np.arange(RW_GD + 128, RW_GD + 160), -np.ones(96, np.int64)]))
    for hp in range(8):
        for base in (RW_R, RW_K, RW_V):
            tiles.append(np.arange(base + hp * 128, base + (hp + 1) * 128))
    return tiles


def att_block_cols():
    blocks = []
    for i in range(4):
        heads = [2 * i, 8 + 2 * i, 2 * i + 1, 9 + 2 * i]
        blocks.append(np.concatenate([np.arange(h * 64, h * 64 + 64) for h in heads]))
    blocks.append(np.arange(1024, 1280))
    return blocks


def mix_row_order():
    rows = []
    for g in range(2):
        for i in range(4):
            for h in (8 * g + i, 8 * g + 4 + i):
                rows.append(np.arange(h * 64, h * 64 + 64))
    rows.append(np.arange(1024, 2048))
    return np.concatenate(rows)


def ktile(w, cols):
    sel = np.where(cols >= 0, cols, 0)
    t = w[:, sel].copy()
    t[:, cols < 0] = 0
    return np.ascontiguousarray(t.reshape(16, 128, len(cols)).transpose(1, 0, 2))


def make_consts():
    c = np.zeros((128, 1024), np.float32)
    c[:, 0:128] = np.eye(128)
    for e in range(2):
        c[e * 64:(e + 1) * 64, 128 + e * 64:128 + (e + 1) * 64] = 1
    p = np.arange(128) % 64
    s = np.arange(64)
    c[:, 256:320] = (p[:, None] > s[None, :])
    c[:, 320:384] = (p[:, None] < s[None, :])
    c[:, 384:448] = (p[:, None] <= s[None, :])
    t = np.arange(512)
    c[:, 448:960] = (t % 64 != 0)[None, :]
    return c


def rope_table(pos):
    half = 32
    inv = 10000.0 ** (-np.arange(half, dtype=np.float64) / half)
    ang = pos.astype(np.float64)[:, None] * inv[None, :]
    cos = np.cos(ang).astype(np.float32)
    sin = np.sin(ang).astype(np.float32)
    return np.concatenate([cos, cos, sin, sin], 1).astype(np.float32)


def prep_shared(inp):
    sh = {}
    w_in = inp["w_in"][0]
    sh["w_att"] = np.stack([ktile(w_in, cols) for cols in att_block_cols()])
    w_rw = w_in[:, 1280:]
    rwt = rw_tile_cols()
    sh["w_rw"] = np.stack([ktile(w_rw, cols) for cols in rwt])
    wo = inp["w_out"][0][mix_row_order(), :]
    sh["w_out"] = np.stack([np.ascontiguousarray(wo[:, i * 512:(i + 1) * 512].reshape(16, 128, 512).transpose(1, 0, 2)) for i in range(4)])
    wu = inp["ffn_w_up"][0]
    upcols = []
    for j in range(44):
        upcols.append(np.arange(j * 128, (j + 1) * 128))
        upcols.append(np.arange(DFF + j * 128, DFF + (j + 1) * 128))
    sh["upcols"] = upcols
    sh["w_up"] = np.stack([ktile(wu, cols) for cols in upcols])
    wd = inp["ffn_w_down"][0]
    sh["w_dn"] = np.ascontiguousarray(wd.reshape(11, 4, 128, D).transpose(0, 2, 1, 3))
    vec = np.zeros((128, 160), np.float32)

    def put(col, v, n):
        vec[:, col:col + n] = np.asarray(v, np.float32).reshape(n, 128).T
    put(0, inp["ln_in_g"], 16)
    put(16, inp["ln_in_b"], 16)
    put(32, inp["ln1_g"][0], 16)
    put(48, inp["ln1_b"][0], 16)
    mu = inp["rw_mu"][0]
    for t, cols in enumerate(rwt):
        sel = np.where(cols >= 0, cols, 0)
        v = mu[sel].copy()
        v[cols < 0] = 0
        vec[:, 64 + t] = v
    put(91, inp["rw_w0"][0], 8)
    put(99, inp["rw_a0"][0], 8)
    put(107, inp["rw_k_k"][0], 8)
    put(115, inp["rw_k_a"][0], 8)
    put(123, inp["rw_r_k"][0].reshape(-1), 8)
    put(131, inp["rw_lnx_g"][0], 8)
    put(139, inp["rw_lnx_b"][0], 8)
    sh["vecs"] = vec
    sh["lora_wa"] = np.concatenate([inp["rw_w2"][0], inp["rw_a2"][0]], 0).astype(np.float32)
    sh["lora_g"] = np.ascontiguousarray(inp["rw_g2"][0][0:128])
    sh["lora_gb"] = np.ascontiguousarray(inp["rw_g2"][0][128:160])
    sh["sinks"] = np.ascontiguousarray(inp["attn_sinks"][0])
    sh["ln2g"] = np.ascontiguousarray(inp["ln2_g"][0])
    sh["ln2b"] = np.ascontiguousarray(inp["ln2_b"][0])
    cw = inp["ffn_conv_w"][0]
    cbias = inp["ffn_conv_b"][0]
    sh["convw"] = np.ascontiguousarray(np.stack([cw[:, cols].T for cols in upcols], 1))
    sh["convb"] = np.ascontiguousarray(np.stack([cbias[cols] for cols in upcols], 1))
    sh["consts"] = make_consts()
    sh["rwt"] = rwt
    return sh


def prep_core(inp, sh, c):
    b, g = c // 2, c % 2
    xp = inp["x_prompt"][b]
    m = {}
    if g == 1:
        m["xseq"] = np.ascontiguousarray(xp)
        pos = np.arange(NT)
    else:
        m["xseq"] = np.ascontiguousarray(np.concatenate([xp[1024:], xp[:1024]], 0))
        pos = np.concatenate([np.arange(1024), np.arange(1024)])
    m["xsmp"] = np.ascontiguousarray(inp["x_sample"][c])
    m["flag"] = np.full((128, 1), float(g), np.float32)
    m["cs_p"] = rope_table(pos)
    m["cs_s"] = rope_table(1024 + np.arange(TS))
    m["cache_k"] = np.ascontiguousarray(inp["cache_k"][0, c].reshape(128, 128))
    m["cache_v"] = np.ascontiguousarray(inp["cache_v"][0, c].reshape(128, 128))
    st = inp["state_wkv"][0, c]
    m["swkv_in"] = np.ascontiguousarray(st.transpose(0, 2, 1).reshape(8, 128, 64))
    sf = inp["state_shift"][0, c, 0]
    t = np.zeros((128, NRW), np.float32)
    for ti, cols in enumerate(sh["rwt"]):
        sel = np.where(cols >= 0, cols, 0)
        v = sf[sel].copy()
        v[cols < 0] = 0
        t[:, ti] = v
    m["sshift_in"] = t
    cvs = inp["state_ffn_conv"][0, c]
    m["sconv_in"] = np.ascontiguousarray(np.stack([cvs[:, cols].T for cols in sh["upcols"]], 1))
    for k in ("w_att", "w_rw", "w_out", "w_up", "w_dn", "vecs", "lora_wa", "lora_g", "lora_gb", "sinks", "ln2g", "ln2b",
              "convw", "convb", "consts"):
        m[k] = sh[k]
    return m


_CACHE = {}


def kernel(**inputs):
    inp = {k: np.asarray(v) for k, v in inputs.items()}
    if "nc" not in _CACHE:
        _CACHE["nc"] = build()[0]
    nc = _CACHE["nc"]
    sh = prep_shared(inp)
    in_maps = [prep_core(inp, sh, c) for c in range(8)]
    res = run_bass_kernel_spmd(nc, in_maps, core_ids=list(range(8)))
    R = res.results
    f32 = np.float32
    y_prompt = np.zeros((4, 2048, D), f32)
    y_sample = np.zeros((8, TS, D), f32)
    p_k = np.zeros((1, 4, 128, 2, 64), f32)
    p_v = np.zeros((1, 4, 128, 2, 64), f32)
    p_wkv = np.zeros((1, 4, 16, 64, 64), f32)
    p_shift = np.zeros((1, 4, 1, 3360), f32)
    p_conv = np.zeros((1, 4, 2, 2 * DFF), f32)
    s_k = np.zeros((1, 8, 128, 2, 64), f32)
    s_v = np.zeros((1, 8, 128, 2, 64), f32)
    s_wkv = np.zeros((1, 8, 16, 64, 64), f32)
    s_shift = np.zeros((1, 8, 1, 3360), f32)
    s_conv = np.zeros((1, 8, 2, 2 * DFF), f32)
    rwt = sh["rwt"]
    upc = sh["upcols"]

    def unwkv(a):
        a = np.asarray(a, f32).reshape(8, 2, 64, 64)
        return a.transpose(0, 1, 3, 2).reshape(16, 64, 64)

    def unshift(a):
        a = np.asarray(a, f32)
        o = np.zeros(3360, f32)
        for t, cols in enumerate(rwt):
            ok = cols >= 0
            o[cols[ok]] = a[ok, t]
        return o

    def unconv(a):
        a = np.asarray(a, f32)
        o = np.zeros((2, 2 * DFF), f32)
        for ft, cols in enumerate(upc):
            o[:, cols] = a[:, ft, :].T
        return o
    for c in range(8):
        b, g = c // 2, c % 2
        r = R[c]
        y_prompt[b, g * 1024:(g + 1) * 1024] = np.asarray(r["y_main"], f32)
        y_sample[c] = np.asarray(r["y_s"], f32)
        if g == 1:
            p_k[0, b] = np.asarray(r["pk"], f32).reshape(128, 2, 64)
            p_v[0, b] = np.asarray(r["pv"], f32).reshape(128, 2, 64)
            p_wkv[0, b] = unwkv(r["pwkv"])
            p_shift[0, b, 0] = unshift(r["pshift"])
            p_conv[0, b] = unconv(r["pconv"])
        s_k[0, c] = np.asarray(r["sk"], f32).reshape(128, 2, 64)
        s_v[0, c] = np.asarray(r["sv"], f32).reshape(128, 2, 64)
        s_wkv[0, c] = unwkv(r["swkv"])
        s_shift[0, c, 0] = unshift(r["sshift"])
        s_conv[0, c] = unconv(r["sconv"])
    return (y_prompt, y_sample, p_k, p_v, p_wkv, p_shift, p_conv, s_k, s_v, s_wkv, s_shift, s_conv)
```
